# Optimizing a Trainium2 kernel written in Bass

```python
import math
import jax, jax.numpy as jnp
from jax import lax
import numpy as np

D_MODEL = 1024
BATCH = 4
SEQ = 8192
DEPTH = 1
DEC_BATCH = 8
DEC_SEQ = 16
PAST_LEN = 2048

CHUNK = 64
Q_BLOCK = 128
HEAD_DIM = 64
FOX_HEADS = 8
DIFF_HEADS = 4
DIFF_VDIM = 2 * HEAD_DIM
FOX_WIDTH = FOX_HEADS * HEAD_DIM
DIFF_QK_WIDTH = DIFF_HEADS * 2 * HEAD_DIM
DIFF_WIDTH = DIFF_HEADS * DIFF_VDIM
MIX_WIDTH = FOX_WIDTH + DIFF_WIDTH
SPLIT_SIZES = (FOX_WIDTH, FOX_WIDTH, FOX_WIDTH, FOX_HEADS, FOX_WIDTH,
               DIFF_QK_WIDTH, DIFF_QK_WIDTH, DIFF_WIDTH, DIFF_WIDTH)
SPLIT_POINTS = tuple(sum(SPLIT_SIZES[:i + 1]) for i in range(len(SPLIT_SIZES) - 1))
IN_WIDTH = sum(SPLIT_SIZES)
DEEPNORM_ALPHA = (2 * DEPTH) ** 0.25
DEEPNORM_BETA = (8 * DEPTH) ** -0.25
FORGET_BIAS_INIT = 3.0
LN_EPS = 1e-5
RMS_EPS = 1e-5
ADA_SCALE = 0.5

kernel_name = "hybrid_fox_diffattn_streaming_step"

F32 = jnp.float32


def layer_norm(x, g, b):
    xf = x.astype(F32)
    mu = jnp.mean(xf, axis=-1, keepdims=True)
    xc = xf - mu
    var = jnp.mean(xc * xc, axis=-1, keepdims=True)
    return (xc * lax.rsqrt(var + LN_EPS) * g.astype(F32) + b.astype(F32)).astype(x.dtype)


def ada_modulation(c, w_ada, b_ada):
    m = jax.nn.silu(c) @ w_ada + b_ada
    shift, scale, gate = jnp.split(m, 3, axis=-1)
    return shift, scale, gate


def in_project(h, w_in, b_f):
    B, T, _ = h.shape
    z = h @ w_in
    fq, fk, fv, ff, fg, dq, dk, dv, dg = jnp.split(z, list(SPLIT_POINTS), axis=-1)

    def heads(t, n, d):
        return t.reshape(B, T, n, d).transpose(0, 2, 1, 3)

    fq = heads(fq, FOX_HEADS, HEAD_DIM)
    fk = heads(fk, FOX_HEADS, HEAD_DIM)
    fv = heads(fv, FOX_HEADS, HEAD_DIM)
    logf = jax.nn.log_sigmoid(ff.astype(F32) + b_f.astype(F32)).transpose(0, 2, 1)
    dq = dq.reshape(B, T, DIFF_HEADS, 2, HEAD_DIM).transpose(0, 2, 1, 3, 4)
    dk = dk.reshape(B, T, DIFF_HEADS, 2, HEAD_DIM).transpose(0, 2, 1, 3, 4)
    dv = heads(dv, DIFF_HEADS, DIFF_VDIM)
    return fq, fk, fv, logf, fg, dq, dk, dv, dg


def fox_attend(q, k, v, cum_q, cum_k, pos_q, pos_k):
    s = jnp.einsum('bhqd,bhkd->bhqk', q, k).astype(F32) * (HEAD_DIM ** -0.5)
    s = s + cum_q[..., :, None] - cum_k[..., None, :]
    mask = pos_k[None, :] <= pos_q[:, None]
    s = jnp.where(mask, s, -jnp.inf)
    p = jax.nn.softmax(s, axis=-1)
    return jnp.einsum('bhqk,bhkd->bhqd', p.astype(v.dtype), v)


def diff_attend(q, k, v, pos_q, pos_k, lam):
    s = jnp.einsum('bhqcd,bhkcd->bhcqk', q, k).astype(F32) * (HEAD_DIM ** -0.5)
    slopes = 2.0 ** (-8.0 * jnp.arange(1, DIFF_HEADS + 1, dtype=F32) / DIFF_HEADS)
    dist = jnp.abs(pos_q[:, None] - pos_k[None, :]).astype(F32)
    s = s - slopes[None, :, None, None, None] * dist
    mask = (pos_k[None, :] // CHUNK) <= (pos_q[:, None] // CHUNK)
    s = jnp.where(mask, s, -jnp.inf)
    p = jax.nn.softmax(s, axis=-1)
    a = p[:, :, 0] - lam * p[:, :, 1]
    return jnp.einsum('bhqk,bhkv->bhqv', a.astype(v.dtype), v)


def diff_lambda(lq1, lk1, lq2, lk2, lam_init):
    return (jnp.exp(jnp.sum(lq1.astype(F32) * lk1.astype(F32)))
            - jnp.exp(jnp.sum(lq2.astype(F32) * lk2.astype(F32))) + lam_init)


def mix_out(fox_o, diff_o, fg, dg, subln_g, lam_init, w_out):
    B, _, T, _ = fox_o.shape
    fox = fox_o.transpose(0, 2, 1, 3).reshape(B, T, FOX_WIDTH)
    df = diff_o.astype(F32)
    df = df * lax.rsqrt(jnp.mean(df * df, axis=-1, keepdims=True) + RMS_EPS)
    df = df * subln_g.astype(F32) * (1.0 - lam_init)
    diff = df.astype(fox.dtype).transpose(0, 2, 1, 3).reshape(B, T, DIFF_WIDTH)
    u = jnp.concatenate([fox * jax.nn.silu(fg), diff * jax.nn.silu(dg)], axis=-1)
    return u @ w_out


def trunk_layer(x, c, p, lam_init, attend):
    shift, scale, gate = ada_modulation(c, p['w_ada'], p['b_ada'])
    h = x * (1.0 + scale[:, None, :]) + shift[:, None, :]
    fq, fk, fv, logf, fg, dq, dk, dv, dg = in_project(h, p['w_in'], p['b_f'])
    lam = diff_lambda(p['lq1'], p['lk1'], p['lq2'], p['lk2'], lam_init)
    fox_o, diff_o = attend(fq, fk, fv, logf, dq, dk, dv, lam)
    branch = mix_out(fox_o, diff_o, fg, dg, p['subln_g'], lam_init, p['w_out'])
    y = layer_norm(DEEPNORM_ALPHA * x + gate[:, None, :] * branch, p['ln_g'], p['ln_b'])
    return y, (fk, fv, logf, dk, dv)


def prompt_attend(fq, fk, fv, logf, dq, dk, dv, lam):
    B, _, S, _ = fq.shape
    nb = S // Q_BLOCK
    pos = jnp.arange(S, dtype=jnp.int32)
    cum = jnp.cumsum(logf, axis=-1)

    def blocks(t):
        t = t.reshape(t.shape[:2] + (nb, Q_BLOCK) + t.shape[3:])
        return jnp.moveaxis(t, 2, 0)

    def unblocks(o):
        o = jnp.moveaxis(o, 0, 2)
        return o.reshape(o.shape[:2] + (S,) + o.shape[4:])

    pos_b = pos.reshape(nb, Q_BLOCK)
    fox_o = lax.map(lambda a: fox_attend(a[0], fk, fv, a[1], cum, a[2], pos),
                    (blocks(fq), blocks(cum), pos_b))
    diff_o = lax.map(lambda a: diff_attend(a[0], dk, dv, a[1], pos, lam),
                     (blocks(dq), pos_b))
    return unblocks(fox_o), unblocks(diff_o)


def make_sample_attend(ck_f, cv_f, clogf, ck_d, cv_d):
    def attend(fq, fk, fv, logf, dq, dk, dv, lam):
        P = ck_f.shape[2]
        T = fq.shape[2]
        fk_all = jnp.concatenate([ck_f.astype(fk.dtype), fk], axis=2)
        fv_all = jnp.concatenate([cv_f.astype(fv.dtype), fv], axis=2)
        cum = jnp.cumsum(jnp.concatenate([clogf.astype(F32), logf], axis=-1), axis=-1)
        dk_all = jnp.concatenate([ck_d.astype(dk.dtype), dk], axis=2)
        dv_all = jnp.concatenate([cv_d.astype(dv.dtype), dv], axis=2)
        pos_k = jnp.arange(P + T, dtype=jnp.int32)
        pos_q = P + jnp.arange(T, dtype=jnp.int32)
        fox_o = fox_attend(fq, fk_all, fv_all, cum[..., P:], cum, pos_q, pos_k)
        diff_o = diff_attend(dq, dk_all, dv_all, pos_q, pos_k, lam)
        return fox_o, diff_o
    return attend


def setup_inputs(seed: int = 0) -> dict:
    key = jax.random.key(seed)
    ks = jax.random.split(key, 24)
    nrm = jax.random.normal
    x_prompt = nrm(ks[0], (BATCH, SEQ, D_MODEL), F32)
    x_sample = nrm(ks[1], (DEC_BATCH, DEC_SEQ, D_MODEL), F32)
    cache_fox_k = nrm(ks[2], (DEPTH, DEC_BATCH, FOX_HEADS, PAST_LEN, HEAD_DIM), F32)
    cache_fox_v = nrm(ks[3], (DEPTH, DEC_BATCH, FOX_HEADS, PAST_LEN, HEAD_DIM), F32)
    cache_fox_logf = jax.nn.log_sigmoid(
        FORGET_BIAS_INIT + nrm(ks[4], (DEPTH, DEC_BATCH, FOX_HEADS, PAST_LEN), F32))
    cache_diff_k = nrm(ks[5], (DEPTH, DEC_BATCH, DIFF_HEADS, PAST_LEN, 2, HEAD_DIM), F32)
    cache_diff_v = nrm(ks[6], (DEPTH, DEC_BATCH, DIFF_HEADS, PAST_LEN, DIFF_VDIM), F32) * DEEPNORM_BETA
    c_prompt = nrm(ks[7], (BATCH, D_MODEL), F32)
    c_sample = nrm(ks[8], (DEC_BATCH, D_MODEL), F32)
    w_ada = nrm(ks[9], (DEPTH, D_MODEL, 3 * D_MODEL), F32) * (ADA_SCALE * D_MODEL ** -0.5)
    b_ada = nrm(ks[10], (DEPTH, 3 * D_MODEL), F32) * 0.02
    col_scale = jnp.concatenate([
        jnp.ones((2 * FOX_WIDTH,), F32),
        jnp.full((FOX_WIDTH,), DEEPNORM_BETA, F32),
        jnp.ones((FOX_HEADS + FOX_WIDTH + 2 * DIFF_QK_WIDTH,), F32),
        jnp.full((DIFF_WIDTH,), DEEPNORM_BETA, F32),
        jnp.ones((DIFF_WIDTH,), F32)])
    w_in = nrm(ks[11], (DEPTH, D_MODEL, IN_WIDTH), F32) * (D_MODEL ** -0.5) * col_scale
    b_f = FORGET_BIAS_INIT + 0.1 * nrm(ks[12], (DEPTH, FOX_HEADS), F32)
    lambda_q1 = 0.1 * nrm(ks[13], (DEPTH, HEAD_DIM), F32)
    lambda_k1 = 0.1 * nrm(ks[14], (DEPTH, HEAD_DIM), F32)
    lambda_q2 = 0.1 * nrm(ks[15], (DEPTH, HEAD_DIM), F32)
    lambda_k2 = 0.1 * nrm(ks[16], (DEPTH, HEAD_DIM), F32)
    subln_g = 1.0 + 0.02 * nrm(ks[17], (DEPTH, DIFF_VDIM), F32)
    w_out = nrm(ks[18], (DEPTH, MIX_WIDTH, D_MODEL), F32) * (MIX_WIDTH ** -0.5) * DEEPNORM_BETA
    ln_g = 1.0 + 0.02 * nrm(ks[19], (DEPTH, D_MODEL), F32)
    ln_b = 0.02 * nrm(ks[20], (DEPTH, D_MODEL), F32)
    return {"x_prompt": x_prompt, "x_sample": x_sample,
            "cache_fox_k": cache_fox_k, "cache_fox_v": cache_fox_v, "cache_fox_logf": cache_fox_logf,
            "cache_diff_k": cache_diff_k, "cache_diff_v": cache_diff_v,
            "c_prompt": c_prompt, "c_sample": c_sample,
            "w_ada": w_ada, "b_ada": b_ada, "w_in": w_in, "b_f": b_f,
            "lambda_q1": lambda_q1, "lambda_k1": lambda_k1, "lambda_q2": lambda_q2, "lambda_k2": lambda_k2,
            "subln_g": subln_g, "w_out": w_out, "ln_g": ln_g, "ln_b": ln_b}


def reference(x_prompt, x_sample, cache_fox_k, cache_fox_v, cache_fox_logf, cache_diff_k, cache_diff_v,
              c_prompt, c_sample, w_ada, b_ada, w_in, b_f, lambda_q1, lambda_k1, lambda_q2, lambda_k2,
              subln_g, w_out, ln_g, ln_b):
    yp = x_prompt
    ys = x_sample
    p_states = ([], [], [], [], [])
    s_states = ([], [], [], [], [])
    for l in range(DEPTH):
        lam_init = 0.8 - 0.6 * math.exp(-0.3 * l)
        p = {'w_ada': w_ada[l], 'b_ada': b_ada[l], 'w_in': w_in[l], 'b_f': b_f[l],
             'lq1': lambda_q1[l], 'lk1': lambda_k1[l], 'lq2': lambda_q2[l], 'lk2': lambda_k2[l],
             'subln_g': subln_g[l], 'w_out': w_out[l], 'ln_g': ln_g[l], 'ln_b': ln_b[l]}
        yp, new_p = trunk_layer(yp, c_prompt, p, lam_init, prompt_attend)
        attend_s = make_sample_attend(cache_fox_k[l], cache_fox_v[l], cache_fox_logf[l],
                                      cache_diff_k[l], cache_diff_v[l])
        ys, new_s = trunk_layer(ys, c_sample, p, lam_init, attend_s)
        for i in range(5):
            p_states[i].append(new_p[i])
            s_states[i].append(new_s[i])
    pk_f, pv_f, plogf, pk_d, pv_d = [jnp.stack(t, axis=0) for t in p_states]
    sk_f, sv_f, slogf, sk_d, sv_d = [jnp.stack(t, axis=0) for t in s_states]
    return (yp, ys, pk_f, pv_f, plogf, pk_d, pv_d, sk_f, sv_f, slogf, sk_d, sv_d)
```

```python
import numpy as np
from contextlib import ExitStack
import concourse.bass as bass
import concourse.mybir as mybir
from concourse.bass_utils import run_bass_kernel_spmd

F32 = mybir.dt.float32
BF16 = mybir.dt.bfloat16
F16 = mybir.dt.float16
AF = mybir.ActivationFunctionType
ALU = mybir.AluOpType

D = 1024
NCORES = 8
PAST = 2048
TS = 16
ALPHA = 2.0 ** 0.25
LAM_INIT = 0.8 - 0.6
SLOPES = [2.0 ** (-8.0 * (i + 1) / 4) for i in range(4)]
NEG = -60000.0


def I(name, *a, **k):
    return lambda e: getattr(e, name)(*a, **k)


class Op:
    __slots__ = ("q", "fn", "reads", "writes", "dma", "semkey", "deps", "inc", "cum", "idx")


class Prog:
    def __init__(self):
        self.ops = []
        self.st = {}

    def _reduce(self, idxs):
        last = {}
        out = set()
        for i in idxs:
            o = self.ops[i]
            if o.dma:
                out.add(i)
            else:
                if o.q not in last or last[o.q] < i:
                    last[o.q] = i
        out.update(last.values())
        return out

    def _add(self, q, fn, reads, writes, dma=False, semkey=None, pwrites=()):
        op = Op()
        op.q, op.fn, op.reads, op.writes, op.dma, op.semkey = q, fn, tuple(reads), tuple(writes) + tuple(pwrites), dma, semkey
        op.inc, op.cum = False, 0
        op.idx = len(self.ops)
        self.ops.append(op)
        deps = set()
        for t in reads:
            st = self.st.setdefault(t, {"w": [], "r": [], "pr": []})
            deps |= self._reduce(st["w"])
        for t in writes:
            st = self.st.setdefault(t, {"w": [], "r": [], "pr": []})
            deps |= self._reduce(st["w"]) | self._reduce(st["r"]) | self._reduce(st["pr"])
        for t in pwrites:
            st = self.st.setdefault(t, {"w": [], "r": [], "pr": []})
            if st["r"]:
                st["pr"], st["r"], st["w"] = st["r"], [], []
            deps |= self._reduce(st["pr"])
        deps.discard(op.idx)
        implied = set()
        for d_ in deps:
            implied.update(self.ops[d_].deps)
        deps -= implied
        op.deps = sorted(deps)
        for t in reads:
            self.st[t]["r"].append(op.idx)
        for t in writes:
            st = self.st[t]
            st["w"], st["r"], st["pr"] = [op.idx], [], []
        for t in pwrites:
            self.st[t]["w"].append(op.idx)
        return op

    def pe(self, fn, reads=(), writes=(), pwrites=()):
        return self._add("pe", fn, reads, writes, pwrites=pwrites)

    def act(self, fn, reads=(), writes=(), pwrites=()):
        return self._add("act", fn, reads, writes, pwrites=pwrites)

    def dve(self, fn, reads=(), writes=(), pwrites=()):
        return self._add("dve", fn, reads, writes, pwrites=pwrites)

    def pool(self, fn, reads=(), writes=(), pwrites=()):
        return self._add("pool", fn, reads, writes, pwrites=pwrites)

    def dma(self, q, fn, semkey, reads=(), writes=(), pwrites=()):
        return self._add(q, fn, reads, writes, dma=True, semkey=semkey, pwrites=pwrites)

    def plan(self):
        ops = self.ops
        for op in ops:
            if op.dma:
                op.inc = True
            for d in op.deps:
                p = ops[d]
                if p.dma:
                    p.inc = True
                elif not (p.q == "pe" and op.q == "pe"):
                    p.inc = True
        cum = {}
        for op in ops:
            if not op.inc:
                continue
            key = ("dma", op.semkey) if op.dma else ("eng", op.q)
            cum[key] = cum.get(key, 0) + (16 if op.dma else 1)
            op.cum = cum[key]
        return list(cum.keys())

    def run(self, sems, q, eng):
        ops = self.ops
        waited = {}
        for op in ops:
            if op.q != q:
                continue
            for d in op.deps:
                p = ops[d]
                if not p.inc:
                    continue
                if (not p.dma) and p.q == "pe" and q == "pe":
                    continue
                key = ("dma", p.semkey) if p.dma else ("eng", p.q)
                if waited.get(key, 0) >= p.cum:
                    continue
                waited[key] = p.cum
                eng.wait_ge(sems[key], p.cum)
            ins = op.fn(eng)
            if op.inc and ins is not None:
                key = ("dma", op.semkey) if op.dma else ("eng", op.q)
                ins.then_inc(sems[key], 16 if op.dma else 1)


def build(S, parts=("sample", "prompt")):
    import os as _os
    SO = S // 2
    NTL = S // 128
    NTO = SO // 128
    NSL = SO // 512
    nc = bass.Bass("TRN2", target_bir_lowering=False)

    def din(name, shape, dt=F32):
        return nc.dram_tensor(name, shape, dt, kind="ExternalInput").ap()

    def dout(name, shape, dt=F32):
        return nc.dram_tensor(name, shape, dt, kind="ExternalOutput").ap()

    xp = din("xp", [S, D])
    cc = din("cc", [128, 16])
    w_ada = din("w_ada", [D, 3 * D])
    b_ada2 = din("b_ada2", [2, 3 * D])
    w_in = din("w_in", [D, 4104])
    bf_bc = din("bf_bc", [128, 8])
    lamv = din("lamv", [1, 256])
    sg_col = din("sg_col", [128, 1])
    lng_bc = din("lng_bc", [128, D])
    lnb_bc = din("lnb_bc", [128, D])
    w_out = din("w_out", [D, D])
    maskf = din("maskf", [128, 8, 512], F16)
    nega = din("nega", [128, 8, 512], F16)
    negi = din("negi", [128, 2, 512], F16)
    l2 = din("l2", [128, 128 + 2 * NSL])
    xsm = din("xsm", [TS, D])
    cfk = din("cfk", [8, PAST, 64])
    cfv = din("cfv", [8, PAST, 64])
    cfl = din("cfl", [8, PAST])
    cdk = din("cdk", [4, PAST, 128])
    cdv = din("cdv", [4, PAST, 128])
    smask = din("smask", [128, TS], F16)
    snega = din("snega", [128, TS], F16)
    snegi = din("snegi", [128, TS], F16)
    l2s = din("l2s", [128, 36])

    y_o = dout("y_o", [SO, D])
    fk_o = dout("fk_o", [8, SO, 64])
    fv_o = dout("fv_o", [8, SO, 64])
    fl_o = dout("fl_o", [8, SO])
    dk_o = dout("dk_o", [4, SO, 128])
    dv_o = dout("dv_o", [4, SO, 128])
    ys_o = dout("ys_o", [TS, D])
    sfk_o = dout("sfk_o", [8, TS, 64])
    sfv_o = dout("sfv_o", [8, TS, 64])
    sfl_o = dout("sfl_o", [8, TS])
    sdk_o = dout("sdk_o", [4, TS, 128])
    sdv_o = dout("sdv_o", [4, TS, 128])

    P = Prog()
    es = ExitStack()

    def sb(name, shape, dt):
        return es.enter_context(nc.sbuf_tensor(name, shape, dt))

    KT = sb("KT", [128, max(S, 8 * D)], BF16)
    VA = sb("VA", [128, max(NTL, 34), 128], BF16)
    QT = sb("QT", [128, SO], BF16)
    GT = sb("GT", [128, SO], BF16)
    UT = sb("UT", [128, 8, SO], BF16)
    WIN = sb("WIN", [128, 8, 516], BF16)
    XS = sb("XS", [128, 3, D], F32)
    HT = sb("HT", [128, 2, 8, 256], BF16)
    PT = sb("PT", [128, 4, 512], BF16)
    SP = sb("SP", [128, 4, 512], F32)
    TAB = sb("TAB", [128, 18, 512], F16)
    RA = sb("RA", [128, 4, 512], F32)
    R0, R1, A0, A1 = RA[:, 0, :], RA[:, 1, :], RA[:, 2, :], RA[:, 3, :]
    LNG = RA[:, 0:2, :].rearrange("p a b -> p (a b)")
    LNB = RA[:, 2:4, :].rearrange("p a b -> p (a b)")
    KVO = sb("KVO", [128, 2, 256], F32)
    FRAW = sb("FRAW", [128, max(NTL, 17), 2], F32)
    LFT = sb("LFT", [128, max(NTL, 17), 2], F32)
    CUMW = sb("CUMW", [128, max(NTL, 17) * 2], F32)
    NCUM = sb("NCUM", [128, max(NTL, 17) * 2 + 2 * NSL + 2], F32)
    CQ = sb("CQ", [128, 2, 512], F32)
    QM = sb("QM", [128, 2, 2, 512], BF16)
    DG = sb("DG", [128, 2, 128], F32)
    TOTB = sb("TOTB", [128, 128], F32)
    L2 = sb("L2", [128, 128 + 2 * NSL], F32)
    IDN = sb("IDN", [128, 128], F32)
    TRI = sb("TRI", [128, 128], F32)
    ONF = sb("ONF", [128, 128], F32)
    EPSC = sb("EPSC", [128, 2], F32)
    ONB = sb("ONB", [128, 128], BF16)
    MOD = sb("MOD", [128, 64], F32)
    GBC = sb("GBC", [128, D], F32)
    CC = sb("CC", [128, 16], F32)
    if SO >= 3 * D:
        M2 = UT[0:2, 0:2, :].rearrange("p a b -> p (a b)")[:, 0:6 * D].bitcast(F32)
    else:
        M2 = sb("M2", [2, 3 * D], F32)
    SEL = sb("SEL", [2, 130], F32)
    BFB = sb("BFB", [128, 8], F32)
    LAMR = sb("LAMR", [1, 256], F32)
    LAMC = sb("LAMC", [128, 4], F32)
    LFO = sb("LFO", [64, 256], F32)
    HTS = sb("HTS", [128, 8, TS], BF16)
    QTS = sb("QTS", [128, TS], BF16)
    GTS = sb("GTS", [128, TS], BF16)
    UTS = sb("UTS", [128, 8, TS], BF16)
    CLT = sb("CLT", [128, 16, 8], F32)
    STB = sb("STB", [128, 3, TS], F16)
    L2S = sb("L2S", [128, 36], F32)
    STAT = sb("STAT", [128, 16], F32)
    BNS = sb("BNS", [128, 4, 6], F32)

    banks = [es.enter_context(nc.psum_tensor("bank%d" % i, [128, 512], F32)) for i in range(8)]
    BK = ["b%d" % i for i in range(8)]

    P.pool(I("memset", IDN[:], 1.0), writes=["IDN"])
    P.pool(I("affine_select", out=IDN[:], in_=IDN[:], pattern=[[1, 128]], compare_op=ALU.is_equal,
                                     fill=0.0, base=0, channel_multiplier=-1), reads=["IDN"], writes=["IDN"])
    P.pool(I("memset", TRI[:], 1.0), writes=["TRI"])
    P.pool(I("affine_select", out=TRI[:], in_=TRI[:], pattern=[[1, 128]], compare_op=ALU.is_ge,
                                     fill=0.0, base=0, channel_multiplier=-1), reads=["TRI"], writes=["TRI"])
    P.pool(I("memset", ONF[:], 1.0), writes=["ONF"])
    P.pool(I("memset", EPSC[:], 1e-5), writes=["EPSC"])
    P.pool(I("memset", ONB[:], 1.0), writes=["ONB"])
    P.pool(I("memset", SEL[:], 0.0), writes=["SEL"])
    P.pool(I("memset", FRAW[:], 0.0), writes=["FRAW"])
    P.pool(I("memset", QM[:].rearrange("p a b c -> p (a b c)"), 0.0), writes=["QM0", "QM1"])
    P.pool(I("memset", VA[:, 16, :], 0.0), writes=["VAs0"])
    P.pool(I("memset", VA[:, 33, :], 0.0), writes=["VAs1"])
    if _os.environ.get('KDBG', '0') == '1' and SO < 3 * D:
        for _ph in range(8):
            P.pool(I("memset", UT[:, _ph, :], 0.0), writes=["UT"])
    P.dma("sp", I("dma_start", out=CC[:], in_=cc[:, :]), "c0", writes=["CC"])
    P.dma("sp", I("dma_start", out=M2[:], in_=b_ada2[:, :]), "c1", writes=["M2", "UT"])
    P.dma("sp", I("dma_start", out=BFB[:], in_=bf_bc[:, :]), "c2", writes=["BFB"])
    P.dma("sp", I("dma_start", out=LAMR[:], in_=lamv[:, :]), "c3", writes=["LAMR"])
    P.dma("sp", I("dma_start", out=LAMC[:, 1:2], in_=sg_col[:, :]), "c4", writes=["LAMC1"])
    P.dve(I("tensor_scalar", out=SEL[:, 0:129], in0=SEL[:, 0:129], scalar1=IDN[0:2, 0:1], scalar2=None, op0=ALU.add),
          reads=["SEL", "IDN"], writes=["SEL"])
    P.dve(I("tensor_scalar", out=SEL[:, 129:130], in0=SEL[:, 129:130], scalar1=IDN[0:2, 1:2], scalar2=None,
                                    op0=ALU.add), reads=["SEL", "IDN"], writes=["SEL"])
    SEL1 = sb("SEL1", [2, 128], F32)
    P.pool(I("memset", SEL1[:], 0.0), writes=["SEL1"])
    P.dve(I("tensor_scalar", out=SEL1[:], in0=SEL1[:], scalar1=IDN[0:2, 1:2], scalar2=None, op0=ALU.add),
          reads=["SEL1", "IDN"], writes=["SEL1"])

    SC = sb("SC", [128, 16], F32)
    P.act(I("activation", out=SC[:], in_=CC[:], func=AF.Silu), reads=["CC"], writes=["SC"])
    SC2 = sb("SC2", [128, 8, 2], F32)
    P.dve(I("tensor_copy", out=SC2[:, :, 0], in_=SC[:, 0:8]), reads=["SC"], writes=["SC2"])
    P.dve(I("tensor_copy", out=SC2[:, :, 1], in_=SC[:, 8:16]), reads=["SC"], writes=["SC2"])
    WA = [SP[:, 0:2, :], SP[:, 2:4, :]]
    wa_v = w_ada.rearrange("(kt p) n -> p kt n", p=128)
    n_ld = 0
    for ch in range(3):
        for kt in range(8):
            buf = n_ld % 2
            n_ld += 1
            P.dma("sp", I("dma_start",
                out=WA[buf].rearrange("p a b -> p (a b)"), in_=wa_v[:, kt, ch * 1024:(ch + 1) * 1024]),
                "wa%d" % buf, writes=["SP%d" % (2 * buf), "SP%d" % (2 * buf + 1)])
            for hf in range(2):
                P.pe(I("matmul",
                    banks[hf][0:2, :], lhsT=SC2[:, kt, :], rhs=WA[buf][:, hf, :], start=(kt == 0), stop=(kt == 7)),
                    reads=["SC2", "SP%d" % (2 * buf), "SP%d" % (2 * buf + 1)], writes=[BK[hf]])
        for hf in range(2):
            c0 = ch * 1024 + hf * 512
            P.dve(I("tensor_tensor", out=M2[:, c0:c0 + 512], in0=banks[hf][0:2, :],
                                                           in1=M2[:, c0:c0 + 512], op=ALU.add),
                  reads=[BK[hf], "M2"], writes=["M2", "UT"])
    for r in range(2):
        for kind in range(2):
            for kt in range(8):
                col = r * 16 + (0 if kind == 1 else 8) + kt
                src0 = kind * 1024 + kt * 128
                P.pe(I("matmul",
                    banks[2][:, col:col + 1], lhsT=M2[:, src0:src0 + 128], rhs=SEL[:, 128 + r:129 + r],
                    start=True, stop=True), reads=["M2", "UT", "SEL"], writes=[BK[2]])
    P.dve(I("tensor_copy", out=MOD[:, 0:32], in_=banks[2][:, 0:32]), reads=[BK[2]], writes=["MOD"])
    for r in range(2):
        P.dve(I("tensor_scalar", out=MOD[:, 16 * r:16 * r + 8], in0=MOD[:, 16 * r:16 * r + 8],
                                             scalar1=1.0, scalar2=None, op0=ALU.add), reads=["MOD"], writes=["MOD"])

    def load_gate_bc(r):
        selr = SEL[:, 0:128] if r == 0 else SEL1[:, :]
        for hf in range(2):
            P.pe(I("matmul", banks[hf][:, :], lhsT=selr, rhs=M2[:, 2048 + hf * 512:2048 + (hf + 1) * 512],
                                           start=True, stop=True), reads=["M2", "UT", "SEL", "SEL1"], writes=[BK[hf]])
            P.dve(I("tensor_copy", out=GBC[:, hf * 512:(hf + 1) * 512], in_=banks[hf][:, :]),
                  reads=[BK[hf]], writes=["GBC"])

    P.dma("sp", I("dma_start", out=L2[:], in_=l2[:, :]), "c5", writes=["L2"])
    P.dma("sp", I("dma_start", out=L2S[:], in_=l2s[:, :]), "c6", writes=["L2S"])
    P.dma("sp", I("dma_start", out=TAB[:, 0:8, :], in_=maskf[:, :, :]), "c7", writes=["TAB"])
    P.dma("sp", I("dma_start", out=TAB[:, 8:16, :], in_=nega[:, :, :]), "c8", pwrites=["TAB"])
    P.dma("sp", I("dma_start", out=TAB[:, 16:18, :], in_=negi[:, :, :]), "c9", pwrites=["TAB"])
    P.dma("sp", I("dma_start", out=STB[:, 0, :], in_=smask[:, :]), "c10", writes=["STB"])
    P.dma("sp", I("dma_start", out=STB[:, 1, :], in_=snega[:, :]), "c11", pwrites=["STB"])
    P.dma("sp", I("dma_start", out=STB[:, 2, :], in_=snegi[:, :]), "c12", pwrites=["STB"])
    LT = sb("LT", [1, 8], F32)
    P.dve(I("tensor_tensor", out=LAMR[:, 0:64], in0=LAMR[:, 0:64], in1=LAMR[:, 64:128], op=ALU.mult),
          reads=["LAMR"], writes=["LAMR"])
    P.dve(I("tensor_tensor", out=LAMR[:, 128:192], in0=LAMR[:, 128:192], in1=LAMR[:, 192:256], op=ALU.mult),
          reads=["LAMR"], writes=["LAMR"])
    P.dve(I("reduce_sum", out=LT[:, 0:1], in_=LAMR[:, 0:64], axis=mybir.AxisListType.X), reads=["LAMR"], writes=["LT"])
    P.dve(I("reduce_sum", out=LT[:, 1:2], in_=LAMR[:, 128:192], axis=mybir.AxisListType.X), reads=["LAMR"], writes=["LT"])
    P.act(I("activation", out=LT[:, 2:4], in_=LT[:, 0:2], func=AF.Exp), reads=["LT"], writes=["LT"])
    P.dve(I("scalar_tensor_tensor", out=LT[:, 4:5], in0=LT[:, 3:4], scalar=-LAM_INIT, in1=LT[:, 2:3],
                                           op0=ALU.add, op1=ALU.subtract), reads=["LT"], writes=["LT"])
    P.pe(I("matmul", banks[3][:, 0:1], lhsT=ONF[0:1, :], rhs=LT[:, 4:5], start=True, stop=True),
         reads=["ONF", "LT"], writes=[BK[3]])
    P.dve(I("tensor_copy", out=LAMC[:, 0:1], in_=banks[3][:, 0:1]), reads=[BK[3]], writes=["LAMC0"])
    P.dve(I("tensor_scalar", out=LAMC[:, 1:2], in0=LAMC[:, 1:2], scalar1=1.0 - LAM_INIT, scalar2=None, op0=ALU.mult),
          reads=["LAMC1"], writes=["LAMC1"])

    win_v = w_in.rearrange("(kt p) c -> p kt c", p=128)

    def phase_cols(ph):
        if ph < 4:
            return dict(k=512 + 128 * ph, v=1024 + 128 * ph, f=1536 + 2 * ph, q=128 * ph, g=1544 + 128 * ph)
        dh = ph - 4
        return dict(k=2568 + 128 * dh, v=3080 + 128 * dh, f=None, q=2056 + 128 * dh, g=3592 + 128 * dh)

    def load_win(ph):
        c = phase_cols(ph)
        lay = [(c["k"], 0, 128), (c["v"], 128, 128), (c["q"], 260, 128), (c["g"], 388, 128)]
        if c["f"] is not None:
            lay.append((c["f"], 256, 2))
        for i, (src, dst, n) in enumerate(lay):
            P.dma("pool", I("dma_start", out=WIN[:, :, dst:dst + n],
                                                                        in_=win_v[:, :, src:src + n]),
                  "win%d" % i, writes=["WINc%d" % (i % 2)], pwrites=["WIN"])

    cnt = {"fm": 0, "tm": 0, "ev": 0}

    def evac_copy(dst, src, reads, writes, scale=None, func=None):
        cnt["ev"] += 1
        if func is not None or (cnt["ev"] % 2 == 0):
            f = func if func is not None else AF.Copy
            if scale is None:
                P.act(I("activation", out=dst, in_=src, func=f), reads, pwrites=writes)
            else:
                P.act(I("activation", out=dst, in_=src, func=f, scale=scale), reads, pwrites=writes)
        else:
            if scale is None:
                P.dve(I("tensor_copy", out=dst, in_=src), reads, pwrites=writes)
            else:
                P.dve(I("tensor_scalar", out=dst, in0=src, scalar1=scale, scalar2=None, op0=ALU.mult), reads, pwrites=writes)

    def transpose_tokens(xsrc_tok, ntok, hdst_fn, htok, modcol, xtok, bankpair):
        for kt in range(8):
            bk = bankpair[kt // 4]
            P.pe(I("transpose", out=banks[bk][:, (kt % 4) * 128:(kt % 4) * 128 + ntok],
                                                     in_=xsrc_tok[:, kt * 128:(kt + 1) * 128],
                                                     identity=IDN[0:ntok, 0:ntok]),
                 reads=[xtok, "IDN"], writes=[BK[bk]])
        for kt in range(8):
            bk = bankpair[kt // 4]
            src = banks[bk][:, (kt % 4) * 128:(kt % 4) * 128 + ntok]
            dst = hdst_fn(kt)
            s1 = MOD[:, modcol + kt:modcol + kt + 1]
            sh = MOD[:, modcol + 8 + kt:modcol + 9 + kt]
            if kt < 4:
                P.act(I("activation", out=dst, in_=src, func=AF.Identity,
                                                                             bias=sh, scale=s1),
                      reads=[BK[bk], "MOD"], pwrites=[htok])
            else:
                P.dve(I("tensor_scalar", out=dst, in0=src, scalar1=s1,
                                                                                scalar2=sh, op0=ALU.mult, op1=ALU.add),
                      reads=[BK[bk], "MOD"], pwrites=[htok])

    def feat_proj(col0, rhs_fn, n, dst, reads, writes, scale=None, func=None):
        bk = 4 + (cnt["fm"] % 2)
        cnt["fm"] += 1
        for kt in range(8):
            P.pe(I("matmul", banks[bk][:, 0:n], lhsT=WIN[:, kt, col0:col0 + 128], rhs=rhs_fn(kt),
                                                  start=(kt == 0), stop=(kt == 7)),
                 reads=["WIN"] + list(reads), writes=[BK[bk]])
        evac_copy(dst, banks[bk][:, 0:n], [BK[bk]], writes, scale=scale, func=func)

    def tok_proj(lhs_fn, ntok, c0, c1, reads):
        bk = 6 + (cnt["tm"] % 2)
        cnt["tm"] += 1
        for kt in range(8):
            P.pe(I("matmul", banks[bk][0:ntok, 0:c1 - c0], lhsT=lhs_fn(kt), rhs=WIN[:, kt, c0:c1],
                                                  start=(kt == 0), stop=(kt == 7)),
                 reads=["WIN"] + list(reads), writes=[BK[bk]])
        return bk

    def logf_and_cum(ph, ntl, l2t, ncols_ref, l2tok):
        n2 = 2 * ntl
        fr = FRAW[:, 0:ntl, :].rearrange("p a b -> p (a b)")
        lf = LFT[:, 0:ntl, :].rearrange("p a b -> p (a b)")
        P.act(I("activation", out=lf, in_=fr, func=AF.Exp, scale=-1.0), reads=["FRAW"], writes=["LFT"])
        P.act(I("activation", out=lf, in_=lf, func=AF.Ln, bias=1.0), reads=["LFT"], writes=["LFT"])
        P.dve(I("tensor_scalar", out=lf, in0=lf, scalar1=-1.0, scalar2=None, op0=ALU.mult), reads=["LFT"], writes=["LFT"])
        return n2, lf

    def cum_from_lft(ntl, l2t, nref, l2tok, valid_last=None):
        n2 = 2 * ntl
        lf = LFT[:, 0:ntl, :].rearrange("p a b -> p (a b)")
        P.pe(I("matmul", banks[0][:, 0:n2], lhsT=TRI[:], rhs=lf, start=True, stop=True),
             reads=["TRI", "LFT"], writes=[BK[0]])
        P.dve(I("tensor_copy", out=CUMW[:, 0:n2], in_=banks[0][:, 0:n2]), reads=[BK[0]], writes=["CUMW"])
        P.pe(I("matmul", banks[1][0:n2, 0:1], lhsT=lf, rhs=ONF[:, 0:1], start=True, stop=True),
             reads=["LFT", "ONF"], writes=[BK[1]])
        P.dve(I("tensor_scalar", out=TOTB[0:n2, :], in0=ONF[0:n2, :], scalar1=banks[1][0:n2, 0:1], scalar2=None,
                                        op0=ALU.mult), reads=[BK[1], "ONF"], writes=["TOTB"])
        ncol = n2 + nref
        P.pe(I("matmul", banks[1][:, 0:ncol], lhsT=TOTB[0:n2, :], rhs=l2t[0:n2, 0:ncol], start=True, stop=True),
             reads=["TOTB", l2tok], writes=[BK[1]])
        P.dve(I("scalar_tensor_tensor", out=NCUM[:, 0:n2], in0=banks[1][:, 0:n2], scalar=-1.0, in1=CUMW[:, 0:n2],
                                               op0=ALU.mult, op1=ALU.subtract), reads=[BK[1], "CUMW"], writes=["NCUM"])
        P.dve(I("tensor_copy", out=NCUM[:, n2:ncol], in_=banks[1][:, n2:ncol]), reads=[BK[1]], writes=["NCUM"])


    cqc = {"n": 0}

    def build_cq(cols_by_hh, n, bank_i):
        for hh in range(2):
            for a, col in enumerate(cols_by_hh[hh]):
                dg = cqc["n"] % 2
                cqc["n"] += 1
                P.act(I("activation", out=DG[0:n, dg, 0:n], in_=IDN[0:n, 0:n], func=AF.Copy, scale=NCUM[0:n, col:col + 1]),
                      reads=["IDN", "NCUM"], writes=["DG%d" % dg])
                P.pe(I("matmul", banks[bank_i][:, hh * 256 + a * n: hh * 256 + (a + 1) * n] if len(cols_by_hh[hh]) * n <= 256
                       else banks[bank_i + hh][:, a * n:(a + 1) * n],
                       lhsT=ONF[0:n, :], rhs=DG[0:n, dg, 0:n], start=True, stop=True),
                     reads=["ONF", "DG%d" % dg], writes=[BK[bank_i], BK[bank_i + 1]])
        w = len(cols_by_hh[0]) * n
        for hh in range(2):
            src = banks[bank_i][:, hh * 256: hh * 256 + w] if w <= 256 else banks[bank_i + hh][:, 0:w]
            P.act(I("activation", out=CQ[:, hh, 0:w], in_=src, func=AF.Copy, scale=-1.0),
                  reads=[BK[bank_i], BK[bank_i + 1]], pwrites=["CQ"])

    acnt = {"n": 0}

    def attention(kind, ph, nq, qT, tiles, gT, uT_dst, slope=None, qtok="QT", gtok="GT", uttok="UT", ktok="KT", vtok="VA",
                  stage="all", qb=None):
        nt = len(tiles)
        if stage in ("all", "prep"):
            qb = acnt["n"] % 2
            acnt["n"] += 1
            P.act(I("activation", out=QM[0:64, qb, 0, 0:nq], in_=qT[0:64, :], func=AF.Copy), reads=[qtok], pwrites=["QM%d" % qb])
            P.act(I("activation", out=QM[64:128, qb, 1, 0:nq], in_=qT[64:128, :], func=AF.Copy), reads=[qtok], pwrites=["QM%d" % qb])
            if stage == "prep":
                return qb
        qmk = "QM%d" % qb

        def score(h):
            st, c = h // 2, h % 2
            tl = tiles[st]
            nk = tl["nk"]
            sbk = h % 4
            P.pe(I("matmul", banks[sbk][0:nk, 0:nq], lhsT=tl["kT"], rhs=QM[:, qb, c, 0:nq], start=True, stop=True),
                 reads=[ktok, qmk], writes=[BK[sbk]])
            pt = PT[0:nk, sbk, 0:nq]
            ptk = "PT%d" % sbk
            sp = SP[0:nk, sbk, 0:nq]
            spk = "SP%d" % sbk
            sb_ap = banks[sbk][0:nk, 0:nq]
            if kind == "fox":
                if tl["mask"] is None:
                    P.dve(I("tensor_tensor", out=sb_ap, in0=sb_ap, in1=tl["cq"][c], op=ALU.add),
                          reads=[BK[sbk], "CQ"], writes=[BK[sbk]])
                    P.act(I("activation", out=pt, in_=sb_ap, func=AF.Exp, bias=tl["bias"][c]),
                          reads=[BK[sbk], "NCUM"], writes=[ptk])
                else:
                    P.dve(I("tensor_tensor", out=sp, in0=sb_ap, in1=tl["cq"][c], op=ALU.add),
                          reads=[BK[sbk], "CQ"], writes=[spk])
                    P.pool(I("tensor_tensor", out=sp, in0=sp, in1=tl["mask"], op=ALU.add),
                           reads=[spk, "TAB", "STB"], writes=[spk])
                    P.act(I("activation", out=pt, in_=sp, func=AF.Exp, bias=tl["bias"][c]),
                          reads=[spk, "NCUM"], writes=[ptk])
            else:
                P.dve(I("scalar_tensor_tensor", out=sb_ap, in0=tl["table"], scalar=slope, in1=sb_ap,
                        op0=ALU.mult, op1=ALU.add), reads=[BK[sbk], "TAB", "STB"], writes=[BK[sbk]])
                P.act(I("activation", out=pt, in_=sb_ap, func=AF.Exp, bias=float(tl["bias"])),
                      reads=[BK[sbk]], writes=[ptk])

        def pv(h):
            st, c = h // 2, h % 2
            tl = tiles[st]
            nk = tl["nk"]
            first, last = (st == 0), (st == nt - 1)
            pt = PT[0:nk, h % 4, 0:nq]
            ptk = "PT%d" % (h % 4)
            P.pe(I("matmul", banks[4 + c][:, 0:nq], lhsT=tl["v"], rhs=pt, start=first, stop=last),
                 reads=[vtok, ptk], writes=[BK[4 + c]])
            P.pe(I("matmul", banks[6 + c][:, 0:nq], lhsT=ONB[0:nk, :], rhs=pt, start=first, stop=last),
                 reads=["ONB", ptk], writes=[BK[6 + c]])

        if stage == "finish":
            pass
        elif nq * nt <= 512:
            W = nq * nt
            for c in range(2):
                sbk = c
                spk = "SP%d" % c
                for st, tl in enumerate(tiles):
                    nk = tl["nk"]
                    P.pe(I("matmul", banks[sbk][0:nk, st * nq:(st + 1) * nq], lhsT=tl["kT"], rhs=QM[:, qb, c, 0:nq],
                           start=True, stop=True), reads=[ktok, qmk], writes=[BK[sbk]])
                for st, tl in enumerate(tiles):
                    nk = tl["nk"]
                    dst = SP[0:nk, c, st * nq:(st + 1) * nq]
                    if kind == "fox":
                        P.dve(I("tensor_scalar", out=dst, in0=tl["cq"][c], scalar1=tl["bias"][c], scalar2=None, op0=ALU.add),
                              reads=["CQ", "NCUM"], pwrites=[spk])
                        if tl["mask"] is not None:
                            P.dve(I("tensor_tensor", out=dst, in0=dst, in1=tl["mask"], op=ALU.add),
                                  reads=[spk, "STB", "TAB"], writes=[spk])
                    else:
                        P.dve(I("tensor_scalar", out=dst, in0=tl["table"], scalar1=float(slope), scalar2=float(tl["bias"]),
                                op0=ALU.mult, op1=ALU.add), reads=["STB", "TAB"], pwrites=[spk])
                P.dve(I("tensor_tensor", out=banks[sbk][:, 0:W], in0=banks[sbk][:, 0:W], in1=SP[:, c, 0:W], op=ALU.add),
                      reads=[BK[sbk], spk], writes=[BK[sbk]])
                P.act(I("activation", out=PT[:, c, 0:W], in_=banks[sbk][:, 0:W], func=AF.Exp),
                      reads=[BK[sbk]], writes=["PT%d" % c])
            for c in range(2):
                for st, tl in enumerate(tiles):
                    nk = tl["nk"]
                    pt = PT[0:nk, c, st * nq:(st + 1) * nq]
                    P.pe(I("matmul", banks[4 + c][:, 0:nq], lhsT=tl["v"], rhs=pt, start=(st == 0), stop=(st == nt - 1)),
                         reads=[vtok, "PT%d" % c], writes=[BK[4 + c]])
                    P.pe(I("matmul", banks[6 + c][:, 0:nq], lhsT=ONB[0:nk, :], rhs=pt, start=(st == 0), stop=(st == nt - 1)),
                         reads=["ONB", "PT%d" % c], writes=[BK[6 + c]])
        else:
            LOOK = 3
            for h in range(2 * nt + LOOK):
                if h < 2 * nt:
                    score(h)
                if h >= LOOK:
                    pv(h - LOOK)
        if stage == "loop":
            return qb
        r0, r1, a0, a1 = R0[:, 0:nq], R1[:, 0:nq], A0[:, 0:nq], A1[:, 0:nq]
        if kind == "fox":
            for c in range(2):
                lo, hi = c * 64, (c + 1) * 64
                P.act(I("activation", out=R0[lo:hi, 0:nq], in_=banks[6 + c][lo:hi, 0:nq], func=AF.Ln), reads=[BK[6 + c]], pwrites=["R0"])
                P.act(I("activation", out=R0[lo:hi, 0:nq], in_=R0[lo:hi, 0:nq], func=AF.Exp, scale=-1.0), reads=["R0"], pwrites=["R0"])
            for c in range(2):
                lo, hi = c * 64, (c + 1) * 64
                P.dve(I("tensor_tensor", out=A0[lo:hi, 0:nq], in0=banks[4 + c][lo:hi, 0:nq], in1=R0[lo:hi, 0:nq], op=ALU.mult),
                      reads=[BK[4 + c], "R0"], pwrites=["A0"])
            P.pool(I("tensor_tensor", out=uT_dst, in0=a0, in1=gT, op=ALU.mult), reads=["A0", gtok], pwrites=[uttok])
        else:
            P.act(I("activation", out=r0, in_=banks[6][:, 0:nq], func=AF.Ln), reads=[BK[6]], writes=["R0"])
            P.act(I("activation", out=r1, in_=banks[7][:, 0:nq], func=AF.Ln), reads=[BK[7]], writes=["R1"])
            P.act(I("activation", out=r0, in_=r0, func=AF.Exp, scale=-1.0), reads=["R0"], writes=["R0"])
            P.act(I("activation", out=r1, in_=r1, func=AF.Exp, scale=-1.0), reads=["R1"], writes=["R1"])
            P.dve(I("tensor_tensor", out=a0, in0=banks[4][:, 0:nq], in1=r0, op=ALU.mult), reads=[BK[4], "R0"], writes=["A0"])
            P.dve(I("tensor_tensor", out=a1, in0=banks[5][:, 0:nq], in1=r1, op=ALU.mult), reads=[BK[5], "R1"], writes=["A1"])
            P.dve(I("scalar_tensor_tensor", out=a0, in0=a1, scalar=LAMC[:, 0:1], in1=a0, op0=ALU.mult, op1=ALU.add),
                  reads=["A0", "A1", "LAMC0"], writes=["A0"])
            P.pool(I("tensor_tensor", out=r0, in0=a0, in1=a0, op=ALU.mult), reads=["A0"], writes=["R0"])
            P.pe(I("matmul", banks[0][:, 0:nq], lhsT=ONF[:], rhs=r0, start=True, stop=True),
                 reads=["ONF", "R0"], writes=[BK[0]])
            P.act(I("activation", out=r1, in_=banks[0][:, 0:nq], func=AF.Ln, scale=1.0 / 128.0, bias=EPSC[:, 0:1]),
                  reads=[BK[0], "EPSC"], writes=["R1"])
            P.act(I("activation", out=r1, in_=r1, func=AF.Exp, scale=-0.5), reads=["R1"], writes=["R1"])
            P.dve(I("tensor_tensor", out=a0, in0=a0, in1=r1, op=ALU.mult), reads=["A0", "R1"], writes=["A0"])
            P.act(I("activation", out=a0, in_=a0, func=AF.Copy, scale=LAMC[:, 1:2]),
                  reads=["A0", "LAMC1"], writes=["A0"])
            P.pool(I("tensor_tensor", out=uT_dst, in0=a0, in1=gT, op=ALU.mult), reads=["A0", gtok], pwrites=[uttok])

    def load_ln_tables():
        P.dma("sp", I("dma_start", out=LNG, in_=lng_bc[:, :]), "c14", writes=["R0", "R1"])
        P.dma("sp", I("dma_start", out=LNB, in_=lnb_bc[:, :]), "c15", writes=["A0", "A1"])

    def load_wout():
        wo_v = w_out.rearrange("(ph p) n -> p ph n", p=128)
        wdst = KT[:, 0:8 * D].rearrange("p (a b) -> p a b", a=8)
        for hf in range(2):
            P.dma("pool", I("dma_start", out=wdst[:, hf * 4:(hf + 1) * 4, :], in_=wo_v[:, hf * 4:(hf + 1) * 4, :]),
                  "wo%d" % hf, writes=(["KT"] if hf == 0 else []), pwrites=([] if hf == 0 else ["KT"]))
        for ph in range(8):
            eng = P.pool if ph % 2 == 0 else P.dve
            eng(I("tensor_tensor", out=wdst[:, ph, :], in0=wdst[:, ph, :], in1=GBC[:, :], op=ALU.mult),
                reads=["KT", "GBC"], pwrites=["KT"])
        return wdst

    opc = {"n": 0}

    def out_proj_ln(ntok, ut_fn, wdst, xtile, xtok, ydst_ap, ytok_sem, ybank, dq="sp"):
        par = opc["n"] % 2
        opc["n"] += 1
        for hf in range(2):
            bk = ybank + hf
            for ph in range(8):
                P.pe(I("matmul", banks[bk][0:ntok, :], lhsT=ut_fn(ph), rhs=wdst[:, ph, hf * 512:(hf + 1) * 512],
                       start=(ph == 0), stop=(ph == 7)), reads=["UT", "UTS", "KT"], writes=[BK[bk]])
        if par == 0:
            Y = SP[0:ntok, 0:2, :].rearrange("p a b -> p (a b)")
            Z = SP[0:ntok, 2:4, :].rearrange("p a b -> p (a b)")
            yt, zt = ["SP0", "SP1"], ["SP2", "SP3"]
        else:
            Y = CQ[0:ntok, :, :].rearrange("p a b -> p (a b)")
            Z = PT[0:ntok, :, :].rearrange("p a b -> p (a b)").bitcast(F32)
            yt, zt = ["CQ"], ["PT0", "PT1", "PT2", "PT3"]
        st0 = 4 * par
        bnt, stt = "BNS%d" % par, "STAT%d" % par
        for hf in range(2):
            bk = ybank + hf
            P.dve(I("scalar_tensor_tensor", out=Y[:, hf * 512:(hf + 1) * 512], in0=xtile[:, hf * 512:(hf + 1) * 512],
                    scalar=ALPHA, in1=banks[bk][0:ntok, :], op0=ALU.mult, op1=ALU.add),
                  reads=[xtok, BK[bk]], pwrites=yt)
        for hf in range(2):
            P.dve(I("bn_stats", out=BNS[0:ntok, 2 * par + hf, :], in_=Y[:, hf * 512:(hf + 1) * 512]),
                  reads=yt, pwrites=[bnt])
        P.dve(I("bn_aggr", out=STAT[0:ntok, st0:st0 + 2], in_=BNS[0:ntok, 2 * par:2 * par + 2, :].rearrange("p a b -> p (a b)")),
              reads=[bnt], writes=[stt])
        P.act(I("activation", out=STAT[0:ntok, st0 + 2:st0 + 3], in_=STAT[0:ntok, st0 + 1:st0 + 2], func=AF.Ln, bias=EPSC[0:ntok, 0:1]),
              reads=[stt, "EPSC"], writes=[stt])
        P.act(I("activation", out=STAT[0:ntok, st0 + 3:st0 + 4], in_=STAT[0:ntok, st0 + 2:st0 + 3], func=AF.Exp, scale=-0.5),
              reads=[stt], writes=[stt])
        P.dve(I("tensor_scalar", out=Y, in0=Y, scalar1=STAT[0:ntok, st0:st0 + 1], scalar2=STAT[0:ntok, st0 + 3:st0 + 4],
                op0=ALU.subtract, op1=ALU.mult), reads=[stt] + yt, writes=yt)
        P.pool(I("tensor_tensor", out=Z, in0=Y, in1=LNG[0:ntok, :], op=ALU.mult), reads=yt + ["R0", "R1"], writes=zt)
        P.pool(I("tensor_tensor", out=Z, in0=Z, in1=LNB[0:ntok, :], op=ALU.add), reads=zt + ["A0", "A1"], writes=zt)
        P.dma(dq, I("dma_start", out=ydst_ap, in_=Z), ytok_sem + str(par), reads=zt, pwrites=["OUT"])

    P.dma("sp", I("dma_start", out=XS[0:TS, 0, :], in_=xsm[:, :]), "xs0", writes=["XS0"])
    transpose_tokens(XS[0:TS, 0, :], TS, lambda kt: HTS[:, kt, :], "HTS", 16, "XS0", (0, 1))
    CLR = XS[0:8, 0:2, :].rearrange("p a b -> p (a b)")
    P.dma("sp", I("dma_start", out=CLR, in_=cfl[:, :]), "c13", writes=["XS0", "XS1"])
    for tl in range(16):
        P.pe(I("transpose", out=banks[2][:, tl * 8:(tl + 1) * 8], in_=CLR[:, tl * 128:(tl + 1) * 128],
                                          identity=IDN[0:8, 0:8]), reads=["XS0", "XS1", "IDN"], writes=[BK[2]])
    P.dve(I("tensor_copy", out=CLT[:].rearrange("p a b -> p (a b)"), in_=banks[2][:, 0:128]), reads=[BK[2]], writes=["CLT"])

    import os as _os
    _sph = [int(v) for v in _os.environ.get('KSPH', '0,1,2,3,4,5,6,7').split(',') if v != '']
    KSTR = 17 * 128

    def s_load_k(ph):
        CK = XS[:, 0:2, :].rearrange("p a b -> p (a b)")
        if ph < 4:
            ckv2 = CK.rearrange("p (t h d) -> p t h d", t=16, h=2)
            for hh in range(2):
                P.dma("sp", I("dma_start", out=ckv2[:, :, hh, :],
                              in_=cfk[2 * ph + hh, :, :].rearrange("(t p) d -> p t d", p=128)),
                      "ck%d_%d" % (hh, ph % 2), writes=["XS0", "XS1"] if hh == 0 else [], pwrites=[] if hh == 0 else ["XS0", "XS1"])
        else:
            ckv = CK.rearrange("p (t d) -> p t d", t=16)
            P.dma("sp", I("dma_start", out=ckv, in_=cdk[ph - 4, :, :].rearrange("(t p) d -> p t d", p=128)),
                  "ck0_%d" % (ph % 2), writes=["XS0", "XS1"])

    def s_load_v(ph):
        vo = 17 * (ph % 2)
        vtok = "VAs%d" % (ph % 2)
        if ph < 4:
            for hh in range(2):
                P.dma("pool", I("dma_start", out=VA[:, vo:vo + 16, hh * 64:(hh + 1) * 64],
                                in_=cfv[2 * ph + hh, :, :].rearrange("(t p) d -> p t d", p=128)),
                      "cv%d_%d" % (hh, ph % 2), pwrites=[vtok])
        else:
            P.dma("pool", I("dma_start", out=VA[:, vo:vo + 16, :], in_=cdv[ph - 4, :, :].rearrange("(t p) d -> p t d", p=128)),
                  "cv0_%d" % (ph % 2), pwrites=[vtok])

    def s_kt(ph):
        ko = KSTR * (ph % 2)
        ktok = "KTs%d" % (ph % 2)
        ckv = XS[:, 0:2, :].rearrange("p a b -> p (a b)").rearrange("p (t e) -> p t e", t=16)
        for g4 in range(4):
            bk = g4 % 2
            for i4 in range(4):
                tl = g4 * 4 + i4
                P.pe(I("transpose", out=banks[bk][:, i4 * 128:(i4 + 1) * 128], in_=ckv[:, tl, :], identity=IDN[:]),
                     reads=["XS0", "XS1", "IDN"], writes=[BK[bk]])
            evac_copy(KT[:, ko + g4 * 512:ko + (g4 + 1) * 512], banks[bk][:, :], [BK[bk]], [ktok])

    def s_proj(ph):
        kind = "fox" if ph < 4 else "diff"
        ko, vo = KSTR * (ph % 2), 17 * (ph % 2)
        ktok, vtok = "KTs%d" % (ph % 2), "VAs%d" % (ph % 2)
        feat_proj(0, lambda kt: HTS[:, kt, :], TS, KT[:, ko + PAST:ko + PAST + TS], ["HTS"], [ktok])
        feat_proj(260, lambda kt: HTS[:, kt, :], TS, QTS[:, :], ["HTS"], ["QTS"], scale=0.125)
        feat_proj(388, lambda kt: HTS[:, kt, :], TS, GTS[:, :], ["HTS"], ["GTS"], func=AF.Silu)
        ncol = 258 if kind == "fox" else 256
        bk = tok_proj(lambda kt: HTS[:, kt, :], TS, 0, ncol, ["HTS"])
        P.dve(I("tensor_copy", out=VA[0:TS, vo + 16, :], in_=banks[bk][0:TS, 128:256]), reads=[BK[bk]], pwrites=[vtok])
        P.dve(I("tensor_copy", out=KVO[0:TS, 0, :], in_=banks[bk][0:TS, 0:256]), reads=[BK[bk]], writes=["KVO0"])
        if kind == "fox":
            P.dma("sp", I("dma_start", out=sfk_o[2 * ph:2 * ph + 2, :, :].rearrange("h t d -> t h d"),
                          in_=KVO[0:TS, 0, 0:128].rearrange("p (h d) -> p h d", h=2)), "so0", reads=["KVO0"], pwrites=["OUT"])
            P.dma("sp", I("dma_start", out=sfv_o[2 * ph:2 * ph + 2, :, :].rearrange("h t d -> t h d"),
                          in_=KVO[0:TS, 0, 128:256].rearrange("p (h d) -> p h d", h=2)), "so1", reads=["KVO0"], pwrites=["OUT"])
            P.dve(I("tensor_tensor", out=FRAW[0:TS, 16, :], in0=banks[bk][0:TS, 256:258],
                    in1=BFB[0:TS, 2 * ph:2 * ph + 2], op=ALU.add), reads=[BK[bk], "BFB"], pwrites=["FRAW"])
        else:
            P.dma("sp", I("dma_start", out=sdk_o[ph - 4, :, :], in_=KVO[0:TS, 0, 0:128]), "so0", reads=["KVO0"], pwrites=["OUT"])
            P.dma("sp", I("dma_start", out=sdv_o[ph - 4, :, :], in_=KVO[0:TS, 0, 128:256]), "so1", reads=["KVO0"], pwrites=["OUT"])

    def s_attn(ph):
        kind = "fox" if ph < 4 else "diff"
        ko, vo = KSTR * (ph % 2), 17 * (ph % 2)
        ktok, vtok = "KTs%d" % (ph % 2), "VAs%d" % (ph % 2)
        if kind == "fox":
            fr = FRAW[0:TS, 16, :]
            lfn = LFT[0:TS, 16, :]
            P.pool(I("memset", LFT[:, 16, :], 0.0), writes=["LFT"])
            P.act(I("activation", out=lfn, in_=fr, func=AF.Exp, scale=-1.0), reads=["FRAW"], writes=["LFT"])
            P.act(I("activation", out=lfn, in_=lfn, func=AF.Ln, bias=1.0), reads=["LFT"], writes=["LFT"])
            P.dve(I("tensor_scalar", out=lfn, in0=lfn, scalar1=-1.0, scalar2=None, op0=ALU.mult), reads=["LFT"], writes=["LFT"])
            P.dve(I("tensor_copy", out=LFT[:, 0:16, :], in_=CLT[:, :, 2 * ph:2 * ph + 2]), reads=["CLT"], writes=["LFT"])
            P.dma("sp", I("dma_start", out=sfl_o[2 * ph:2 * ph + 2, :].rearrange("h t -> t h"), in_=LFT[0:TS, 16, :],
                          allow_slow_non_contiguous=True), "so2", reads=["LFT"], pwrites=["OUT"])
            cum_from_lft(17, L2S, 2, "L2S")
            build_cq([[32], [33]], TS, 0)
        tiles = []
        for tl in range(17):
            nk = 128 if tl < 16 else TS
            d = dict(kT=KT[:, ko + tl * 128:ko + tl * 128 + nk], v=VA[0:nk, vo + tl, :], nk=nk, mask=None, table=None)
            if kind == "fox":
                d["bias"] = [NCUM[0:nk, 2 * tl:2 * tl + 1], NCUM[0:nk, 2 * tl + 1:2 * tl + 2]]
                d["cq"] = [CQ[0:nk, 0, 0:TS], CQ[0:nk, 1, 0:TS]]
                if tl == 16:
                    d["mask"] = STB[0:TS, 0, :]
            else:
                if tl < 16:
                    d["table"] = STB[:, 2, :]
                    d["bias"] = -SLOPES[ph - 4] * (PAST - 128 * tl)
                else:
                    d["table"] = STB[0:TS, 1, :]
                    d["bias"] = 0.0
            tiles.append(d)
        attention(kind, ph, TS, QTS[:, :], tiles, GTS[:, :], UTS[:, ph, :], slope=(SLOPES[ph - 4] if ph >= 4 else None),
                  qtok="QTS", gtok="GTS", uttok="UTS", ktok=ktok, vtok=vtok)

    if "sample" in parts:
        load_win(0)
        s_load_k(0)
        s_load_v(0)
        for ph in range(8):
            s_kt(ph)
            if ph + 1 < 8:
                s_load_k(ph + 1)
                s_load_v(ph + 1)
            s_proj(ph)
            if ph + 1 < 8:
                load_win(ph + 1)
            s_attn(ph)
        P.dve(I("memset", STAT[:, 15:16], 0.0), reads=["KTs0", "KTs1", "VAs0", "VAs1"], writes=["KT", "VA"])

    load_gate_bc(1)
    wdst = load_wout()
    load_ln_tables()
    P.dma("sp", I("dma_start", out=XS[0:TS, 0, :], in_=xsm[:, :]), "xs0", writes=["XS0"])
    out_proj_ln(TS, lambda ph: UTS[:, ph, :], wdst, XS[0:TS, 0, :], "XS0", ys_o[:, :], "ys", 0)

    load_gate_bc(0)
    NSTEP = S // 256
    _pph = [int(v) for v in _os.environ.get('KPPH', '0,1,2,3,4,5,6,7').split(',') if v != '']

    def kv_prefetch(ph_):
        load_win(ph_)
        for n0 in range(2):
            P.dma("sp", I("dma_start", out=XS[:, n0, :], in_=xp[n0 * 128:(n0 + 1) * 128, :]), "xs%d" % n0, writes=["XS%d" % n0])

    for ph in (_pph if "prompt" in parts else []):
        kind = "fox" if ph < 4 else "diff"
        if ph == _pph[0]:
            kv_prefetch(ph)
        ncol = 258 if kind == "fox" else 256

        def kv_T(step):
            hb = step % 2
            htok = "HT%d" % hb
            for sub in range(2):
                ns_ = 2 * step + sub
                xb = ns_ % 3
                nxt = ns_ + 2
                if nxt < 2 * NSTEP:
                    P.dma("sp", I("dma_start", out=XS[:, nxt % 3, :], in_=xp[nxt * 128:(nxt + 1) * 128, :]),
                          "xs%d" % (nxt % 3), writes=["XS%d" % (nxt % 3)])
                bp = ns_ % 2
                transpose_tokens(XS[:, xb, :], 128, lambda kt, hb=hb, sub=sub: HT[:, hb, kt, sub * 128:(sub + 1) * 128],
                                 htok, 0, "XS%d" % xb, (2 * bp, 2 * bp + 1))

        def kv_proj(step):
            own = step < NSTEP // 2
            hb = step % 2
            htok = "HT%d" % hb
            tok0 = step * 256
            feat_proj(0, lambda kt, hb=hb: HT[:, hb, kt, :], 256, KT[:, tok0:tok0 + 256], [htok], ["KT"])
            if own:
                feat_proj(260, lambda kt, hb=hb: HT[:, hb, kt, :], 256, QT[:, tok0:tok0 + 256], [htok], ["QT"], scale=0.125)
                feat_proj(388, lambda kt, hb=hb: HT[:, hb, kt, :], 256, GT[:, tok0:tok0 + 256], [htok], ["GT"], func=AF.Silu)
            for sub in range(2):
                tile_i = step * 2 + sub
                t0 = tok0 + sub * 128
                c0 = 0 if own else 128
                bk = tok_proj(lambda kt, hb=hb, sub=sub: HT[:, hb, kt, sub * 128:(sub + 1) * 128], 128, c0, ncol, [htok])
                voff = 128 - c0
                P.dve(I("tensor_copy", out=VA[:, tile_i, :], in_=banks[bk][:, voff:voff + 128]),
                      reads=[BK[bk]], pwrites=["VA"])
                if kind == "fox":
                    P.dve(I("tensor_tensor", out=FRAW[:, tile_i, :], in0=banks[bk][:, voff + 128:voff + 130],
                            in1=BFB[:, 2 * ph:2 * ph + 2], op=ALU.add), reads=[BK[bk], "BFB"], pwrites=["FRAW"])
                if own:
                    kb = tile_i % 2
                    kvt = "KVO%d" % kb
                    P.dve(I("tensor_copy", out=KVO[:, kb, :], in_=banks[bk][:, 0:256]), reads=[BK[bk]], writes=[kvt])
                    if kind == "fox":
                        P.dma("pool", I("dma_start", out=fk_o[2 * ph:2 * ph + 2, t0:t0 + 128, :].rearrange("h t d -> t h d"),
                                        in_=KVO[:, kb, 0:128].rearrange("p (h d) -> p h d", h=2)), "ko%d" % kb, reads=[kvt], pwrites=["OUT"])
                        P.dma("pool", I("dma_start", out=fv_o[2 * ph:2 * ph + 2, t0:t0 + 128, :].rearrange("h t d -> t h d"),
                                        in_=KVO[:, kb, 128:256].rearrange("p (h d) -> p h d", h=2)), "vo%d" % kb, reads=[kvt], pwrites=["OUT"])
                    else:
                        P.dma("pool", I("dma_start", out=dk_o[ph - 4, t0:t0 + 128, :], in_=KVO[:, kb, 0:128]),
                              "ko%d" % kb, reads=[kvt], pwrites=["OUT"])
                        P.dma("pool", I("dma_start", out=dv_o[ph - 4, t0:t0 + 128, :], in_=KVO[:, kb, 128:256]),
                              "vo%d" % kb, reads=[kvt], pwrites=["OUT"])

        kv_T(0)
        for step in range(NSTEP):
            if step + 1 < NSTEP:
                kv_T(step + 1)
            kv_proj(step)
        if kind == "fox":
            logf_and_cum(ph, NTL, L2, 2 * NSL, "L2")
            cum_from_lft(NTL, L2, 2 * NSL, "L2")
            for hh in range(2):
                P.pe(I("transpose", out=banks[2][0:NTO, hh * 128:(hh + 1) * 128], in_=LFT[:, 0:NTO, hh], identity=IDN[:]),
                     reads=["LFT", "IDN"], writes=[BK[2]])
            P.dve(I("tensor_copy", out=LFO[0:NTO, :], in_=banks[2][0:NTO, 0:256]), reads=[BK[2]], writes=["LFO"])
            for hh in range(2):
                P.dma("sp", I("dma_start", out=fl_o[2 * ph + hh, :].rearrange("(t p) -> t p", p=128),
                              in_=LFO[0:NTO, hh * 128:(hh + 1) * 128]), "lfo%d" % hh, reads=["LFO"], pwrites=["OUT"])
        _nx = _pph.index(ph) + 1
        if _nx < len(_pph):
            kv_prefetch(_pph[_nx])
        slope_ = (SLOPES[ph - 4] if ph >= 4 else None)

        def slot_tiles(i):
            tiles = []
            past = [to for to in range(4 * i)] + [NTO + to for to in range(4 * i)]
            zone = [4 * i + z for z in range(4)] + [NTO + 4 * i + z for z in range(4)]
            for tl in past:
                d = dict(kT=KT[:, tl * 128:(tl + 1) * 128], v=VA[:, tl, :], nk=128, mask=None, table=None)
                if kind == "fox":
                    d["bias"] = [NCUM[:, 2 * tl:2 * tl + 1], NCUM[:, 2 * tl + 1:2 * tl + 2]]
                    d["cq"] = [CQ[:, 0, :], CQ[:, 1, :]]
                else:
                    to = tl % NTO
                    d["table"] = TAB[:, 16 + (0 if tl < NTO else 1), :]
                    d["bias"] = -SLOPES[ph - 4] * (1024 * i - 512 * (to // 2) - 128 * (to % 2))
                tiles.append(d)
            for z, tl in enumerate(zone):
                d = dict(kT=KT[:, tl * 128:(tl + 1) * 128], v=VA[:, tl, :], nk=128, mask=None, table=None)
                if kind == "fox":
                    d["bias"] = [NCUM[:, 2 * tl:2 * tl + 1], NCUM[:, 2 * tl + 1:2 * tl + 2]]
                    d["cq"] = [CQ[:, 0, :], CQ[:, 1, :]]
                    d["mask"] = TAB[:, z, :]
                else:
                    d["table"] = TAB[:, 8 + z, :]
                    d["bias"] = 0.0
                tiles.append(d)
            return tiles

        def slot_prep(i):
            if kind == "fox":
                build_cq([[(4 * i + a_) * 2 + hh_ for a_ in range(4)] for hh_ in range(2)], 128, 0)
            return attention(kind, ph, 512, QT[:, i * 512:(i + 1) * 512], [], None, None, stage="prep")

        qb_ = slot_prep(0)
        for i in range(NSL):
            q0 = i * 512
            tiles = slot_tiles(i)
            attention(kind, ph, 512, QT[:, q0:q0 + 512], tiles, GT[:, q0:q0 + 512], UT[:, ph, q0:q0 + 512],
                      slope=slope_, stage="loop", qb=qb_)
            qb_next = slot_prep(i + 1) if i + 1 < NSL else None
            attention(kind, ph, 512, QT[:, q0:q0 + 512], tiles, GT[:, q0:q0 + 512], UT[:, ph, q0:q0 + 512],
                      slope=slope_, stage="finish", qb=qb_)
            qb_ = qb_next

    wdst = load_wout()
    load_ln_tables()
    _fin = (NTO if "prompt" in parts else 0)
    for t0_ in range(min(2, _fin)):
        P.dma("sp", I("dma_start", out=XS[:, t0_ % 3, :], in_=xp[t0_ * 128:(t0_ + 1) * 128, :]),
              "xs%d" % (t0_ % 3), writes=["XS%d" % (t0_ % 3)])
    for tt in range(_fin):
        xb = tt % 3
        if tt + 2 < _fin:
            nb_ = (tt + 2) % 3
            P.dma("sp", I("dma_start", out=XS[:, nb_, :], in_=xp[(tt + 2) * 128:(tt + 3) * 128, :]),
                  "xs%d" % nb_, writes=["XS%d" % nb_])
        out_proj_ln(128, lambda ph, tt=tt: UT[:, ph, tt * 128:(tt + 1) * 128], wdst, XS[:, xb, :], "XS%d" % xb,
                    y_o[tt * 128:(tt + 1) * 128, :], "yo", 2 * (tt % 2), dq="pool")

    P.dma("sp", None, "fence", reads=["OUT"])
    fence = P.ops[-1]
    fence.dma = False
    fence.fn = lambda q: None
    fence.deps = sorted(set(fence.deps) | {op.idx for op in P.ops if "OUT" in op.writes})

    print('[build] sbuf bytes remaining', nc.sbuf_bytes_remaining, 'ops', len(P.ops))
    keys = P.plan()
    sems = {k: es.enter_context(nc.semaphore("s_%s_%s" % (k[0], k[1]))) for k in keys}
    block = es.enter_context(nc.Block())

    @block.sync
    def _(q):
        P.run(sems, "sp", q)

    @block.tensor
    def _(q):
        P.run(sems, "pe", q)

    @block.scalar
    def _(q):
        P.run(sems, "act", q)

    @block.vector
    def _(q):
        P.run(sems, "dve", q)

    @block.gpsimd
    def _(q):
        P.run(sems, "pool", q)

    es.close()
    return nc


def _tables(j, S):
    NSL = (S // 2) // 512
    NTO = (S // 2) // 128
    NTL = S // 128
    t = np.arange(512)
    tq = 512 * (t // 256) + 256 * j + (t % 256)
    p = np.arange(128)[:, None]
    maskf = np.zeros((128, 8, 512), np.float16)
    nega = np.zeros((128, 8, 512), np.float16)
    for z in range(8):
        zz = z % 4
        half = j if z < 4 else 1 - j
        pos = 512 * (zz // 2) + 256 * half + 128 * (zz % 2)
        s = pos + p
        maskf[:, z, :] = np.where(s <= tq[None, :], 0.0, NEG).astype(np.float16)
        ok = (s // 64) <= (tq[None, :] // 64)
        nega[:, z, :] = np.where(ok, -np.abs(tq[None, :] - s), NEG).astype(np.float16)
    negi = np.zeros((128, 2, 512), np.float16)
    negi[:, 0, :] = -(tq[None, :] - 256 * j - p)
    negi[:, 1, :] = -(tq[None, :] - 256 * (1 - j) - p)

    def nat(tl):
        own = tl < NTO
        to = tl % NTO
        half = j if own else 1 - j
        return 4 * (to // 2) + 2 * half + (to % 2)

    ncol = 128 + 2 * NSL
    l2 = np.zeros((128, ncol), np.float32)
    if 2 * NTL <= 128:
        for tl1 in range(NTL):
            for hh in range(2):
                for tl2 in range(NTL):
                    if nat(tl1) < nat(tl2):
                        l2[tl1 * 2 + hh, tl2 * 2 + hh] = 1.0
                for i in range(NSL):
                    if nat(tl1) < 8 * i:
                        l2[tl1 * 2 + hh, 2 * NTL + 2 * i + hh] = 1.0
    return maskf, nega, negi, l2


def _sample_tables():
    p = np.arange(128)[:, None]
    t = np.arange(TS)[None, :]
    smask = np.where(p <= t, 0.0, NEG)
    snega = (-np.abs(t - p)).astype(np.float16)
    snegi = (-(t - p)).astype(np.float16)
    l2s = np.zeros((128, 36), np.float32)
    for t1 in range(17):
        for hh in range(2):
            for t2 in range(17):
                if t1 < t2:
                    l2s[t1 * 2 + hh, t2 * 2 + hh] = 1.0
            if t1 < 16:
                l2s[t1 * 2 + hh, 34 + hh] = 1.0
    return smask.astype(np.float16), snega, snegi, l2s


_NC_CACHE = {}


def kernel(x_prompt, x_sample, cache_fox_k, cache_fox_v, cache_fox_logf, cache_diff_k, cache_diff_v,
           c_prompt, c_sample, w_ada, b_ada, w_in, b_f, lambda_q1, lambda_k1, lambda_q2, lambda_k2,
           subln_g, w_out, ln_g, ln_b):
    f = lambda a: np.ascontiguousarray(np.asarray(a, dtype=np.float32))
    x_prompt, x_sample = f(x_prompt), f(x_sample)
    B, S, _ = x_prompt.shape
    SO = S // 2
    if S not in _NC_CACHE:
        import os
        _NC_CACHE[S] = build(S, tuple(os.environ.get('KPARTS', 'sample,prompt').split(',')))
    nc = _NC_CACHE[S]
    smask, snega, snegi, l2s = _sample_tables()
    in_maps = []
    perms = []
    for c in range(NCORES):
        b, j = c // 2, c % 2
        blk = np.arange(S).reshape(S // 512, 2, 256)
        own = blk[:, j, :].reshape(-1)
        oth = blk[:, 1 - j, :].reshape(-1)
        perms.append(own)
        maskf, nega, negi, l2 = _tables(j, S)
        cc = np.concatenate([f(c_prompt)[b].reshape(8, 128).T, f(c_sample)[c].reshape(8, 128).T], axis=1)
        lamv = np.concatenate([f(lambda_q1)[0], f(lambda_k1)[0], f(lambda_q2)[0], f(lambda_k2)[0]])[None, :]
        in_maps.append({
            "xp": np.ascontiguousarray(x_prompt[b][np.concatenate([own, oth])]),
            "cc": np.ascontiguousarray(cc),
            "w_ada": f(w_ada)[0], "b_ada2": np.ascontiguousarray(np.repeat(f(b_ada), 2, axis=0)),
            "w_in": f(w_in)[0], "bf_bc": np.ascontiguousarray(np.repeat(f(b_f), 128, axis=0)),
            "lamv": np.ascontiguousarray(lamv), "sg_col": np.ascontiguousarray(f(subln_g)[0][:, None]),
            "lng_bc": np.ascontiguousarray(np.repeat(f(ln_g), 128, axis=0)),
            "lnb_bc": np.ascontiguousarray(np.repeat(f(ln_b), 128, axis=0)),
            "w_out": f(w_out)[0], "maskf": maskf, "nega": nega, "negi": negi, "l2": l2,
            "xsm": x_sample[c], "cfk": f(cache_fox_k)[0, c], "cfv": f(cache_fox_v)[0, c], "cfl": f(cache_fox_logf)[0, c],
            "cdk": np.ascontiguousarray(f(cache_diff_k)[0, c].reshape(4, PAST, 128)), "cdv": f(cache_diff_v)[0, c],
            "smask": smask, "snega": snega, "snegi": snegi, "l2s": l2s,
        })
    res = run_bass_kernel_spmd(nc, in_maps, core_ids=list(range(NCORES)))
    R = res.results
    yp = np.zeros((B, S, D), np.float32)
    pk_f = np.zeros((1, B, 8, S, 64), np.float32)
    pv_f = np.zeros((1, B, 8, S, 64), np.float32)
    plf = np.zeros((1, B, 8, S), np.float32)
    pk_d = np.zeros((1, B, 4, S, 2, 64), np.float32)
    pv_d = np.zeros((1, B, 4, S, 128), np.float32)
    ys = np.zeros((NCORES, TS, D), np.float32)
    sk_f = np.zeros((1, NCORES, 8, TS, 64), np.float32)
    sv_f = np.zeros((1, NCORES, 8, TS, 64), np.float32)
    slf = np.zeros((1, NCORES, 8, TS), np.float32)
    sk_d = np.zeros((1, NCORES, 4, TS, 2, 64), np.float32)
    sv_d = np.zeros((1, NCORES, 4, TS, 128), np.float32)
    for c in range(NCORES):
        b, own = c // 2, perms[c]
        r = R[c]
        yp[b, own] = r["y_o"]
        pk_f[0, b][:, own] = r["fk_o"]
        pv_f[0, b][:, own] = r["fv_o"]
        plf[0, b][:, own] = r["fl_o"]
        pk_d[0, b][:, own] = r["dk_o"].reshape(4, SO, 2, 64)
        pv_d[0, b][:, own] = r["dv_o"]
        ys[c] = r["ys_o"]
        sk_f[0, c] = r["sfk_o"]
        sv_f[0, c] = r["sfv_o"]
        slf[0, c] = r["sfl_o"]
        sk_d[0, c] = r["sdk_o"].reshape(4, TS, 2, 64)
        sv_d[0, c] = r["sdv_o"]
    return (yp, ys, pk_f, pv_f, plf, pk_d, pv_d, sk_f, sv_f, slf, sk_d, sv_d)
```

```python
import numpy as np
from contextlib import ExitStack
import concourse.bass as bass
import concourse.mybir as mybir
from concourse.bass_utils import run_bass_kernel_spmd

F32 = mybir.dt.float32
BF16 = mybir.dt.bfloat16
F16 = mybir.dt.float16
AF = mybir.ActivationFunctionType
ALU = mybir.AluOpType

D = 1024
NCORES = 8
PAST = 2048
TS = 16
ALPHA = 2.0 ** 0.25
LAM_INIT = 0.8 - 0.6
SLOPES = [2.0 ** (-8.0 * (i + 1) / 4) for i in range(4)]
NEG = -60000.0


def I(name, *a, **k):
    return lambda e: getattr(e, name)(*a, **k)


class Op:
    __slots__ = ("q", "fn", "reads", "writes", "dma", "semkey", "deps", "inc", "cum", "idx")


class Prog:
    def __init__(self):
        self.ops = []
        self.st = {}

    def _reduce(self, idxs):
        last = {}
        out = set()
        for i in idxs:
            o = self.ops[i]
            if o.dma:
                out.add(i)
            else:
                if o.q not in last or last[o.q] < i:
                    last[o.q] = i
        out.update(last.values())
        return out

    def _add(self, q, fn, reads, writes, dma=False, semkey=None, pwrites=()):
        op = Op()
        op.q, op.fn, op.reads, op.writes, op.dma, op.semkey = q, fn, tuple(reads), tuple(writes) + tuple(pwrites), dma, semkey
        op.inc, op.cum = False, 0
        op.idx = len(self.ops)
        self.ops.append(op)
        deps = set()
        for t in reads:
            st = self.st.setdefault(t, {"w": [], "r": [], "pr": []})
            deps |= self._reduce(st["w"])
        for t in writes:
            st = self.st.setdefault(t, {"w": [], "r": [], "pr": []})
            deps |= self._reduce(st["w"]) | self._reduce(st["r"]) | self._reduce(st["pr"])
        for t in pwrites:
            st = self.st.setdefault(t, {"w": [], "r": [], "pr": []})
            if st["r"]:
                st["pr"], st["r"], st["w"] = st["r"], [], []
            deps |= self._reduce(st["pr"])
        deps.discard(op.idx)
        implied = set()
        for d_ in deps:
            implied.update(self.ops[d_].deps)
        deps -= implied
        op.deps = sorted(deps)
        for t in reads:
            self.st[t]["r"].append(op.idx)
        for t in writes:
            st = self.st[t]
            st["w"], st["r"], st["pr"] = [op.idx], [], []
        for t in pwrites:
            self.st[t]["w"].append(op.idx)
        return op

    def pe(self, fn, reads=(), writes=(), pwrites=()):
        return self._add("pe", fn, reads, writes, pwrites=pwrites)

    def act(self, fn, reads=(), writes=(), pwrites=()):
        return self._add("act", fn, reads, writes, pwrites=pwrites)

    def dve(self, fn, reads=(), writes=(), pwrites=()):
        return self._add("dve", fn, reads, writes, pwrites=pwrites)

    def pool(self, fn, reads=(), writes=(), pwrites=()):
        return self._add("pool", fn, reads, writes, pwrites=pwrites)

    def dma(self, q, fn, semkey, reads=(), writes=(), pwrites=()):
        return self._add(q, fn, reads, writes, dma=True, semkey=semkey, pwrites=pwrites)

    def plan(self):
        ops = self.ops
        for op in ops:
            if op.dma:
                op.inc = True
            for d in op.deps:
                p = ops[d]
                if p.dma:
                    p.inc = True
                elif not (p.q == "pe" and op.q == "pe"):
                    p.inc = True
        cum = {}
        for op in ops:
            if not op.inc:
                continue
            key = ("dma", op.semkey) if op.dma else ("eng", op.q)
            cum[key] = cum.get(key, 0) + (16 if op.dma else 1)
            op.cum = cum[key]
        return list(cum.keys())

    def run(self, sems, q, eng):
        ops = self.ops
        waited = {}
        for op in ops:
            if op.q != q:
                continue
            for d in op.deps:
                p = ops[d]
                if not p.inc:
                    continue
                if (not p.dma) and p.q == "pe" and q == "pe":
                    continue
                key = ("dma", p.semkey) if p.dma else ("eng", p.q)
                if waited.get(key, 0) >= p.cum:
                    continue
                waited[key] = p.cum
                eng.wait_ge(sems[key], p.cum)
            ins = op.fn(eng)
            if op.inc and ins is not None:
                key = ("dma", op.semkey) if op.dma else ("eng", op.q)
                ins.then_inc(sems[key], 16 if op.dma else 1)


def build(S, parts=("sample", "prompt")):
    import os as _os
    SO = S // 2
    NTL = S // 128
    NTO = SO // 128
    NSL = SO // 512
    nc = bass.Bass("TRN2", target_bir_lowering=False)

    def din(name, shape, dt=F32):
        return nc.dram_tensor(name, shape, dt, kind="ExternalInput").ap()

    def dout(name, shape, dt=F32):
        return nc.dram_tensor(name, shape, dt, kind="ExternalOutput").ap()

    xp = din("xp", [S, D])
    cc = din("cc", [128, 16])
    w_ada = din("w_ada", [D, 3 * D])
    b_ada2 = din("b_ada2", [2, 3 * D])
    w_in = din("w_in", [D, 4104])
    bf_bc = din("bf_bc", [128, 8])
    lamv = din("lamv", [1, 256])
    sg_col = din("sg_col", [128, 1])
    lng_bc = din("lng_bc", [128, D])
    lnb_bc = din("lnb_bc", [128, D])
    w_out = din("w_out", [D, D])
    maskf = din("maskf", [128, 8, 512], F16)
    nega = din("nega", [128, 8, 512], F16)
    negi = din("negi", [128, 2, 512], F16)
    l2 = din("l2", [128, 128 + 2 * NSL])
    xsm = din("xsm", [TS, D])
    cfk = din("cfk", [8, PAST, 64])
    cfv = din("cfv", [8, PAST, 64])
    cfl = din("cfl", [8, PAST])
    cdk = din("cdk", [4, PAST, 128])
    cdv = din("cdv", [4, PAST, 128])
    smask = din("smask", [128, TS], F16)
    snega = din("snega", [128, TS], F16)
    snegi = din("snegi", [128, TS], F16)
    l2s = din("l2s", [128, 36])

    y_o = dout("y_o", [SO, D])
    fk_o = dout("fk_o", [8, SO, 64])
    fv_o = dout("fv_o", [8, SO, 64])
    fl_o = dout("fl_o", [8, SO])
    dk_o = dout("dk_o", [4, SO, 128])
    dv_o = dout("dv_o", [4, SO, 128])
    ys_o = dout("ys_o", [TS, D])
    sfk_o = dout("sfk_o", [8, TS, 64])
    sfv_o = dout("sfv_o", [8, TS, 64])
    sfl_o = dout("sfl_o", [8, TS])
    sdk_o = dout("sdk_o", [4, TS, 128])
    sdv_o = dout("sdv_o", [4, TS, 128])

    P = Prog()
    es = ExitStack()

    def sb(name, shape, dt):
        return es.enter_context(nc.sbuf_tensor(name, shape, dt))

    KT = sb("KT", [128, max(S, 8 * D)], BF16)
    VA = sb("VA", [128, max(NTL, 34), 128], BF16)
    QT = sb("QT", [128, SO], BF16)
    GT = sb("GT", [128, SO], BF16)
    UT = sb("UT", [128, 8, SO], BF16)
    WIN = sb("WIN", [128, 8, 516], BF16)
    XS = sb("XS", [128, 3, D], F32)
    HT = sb("HT", [128, 2, 8, 256], BF16)
    PT = sb("PT", [128, 4, 512], BF16)
    SP = sb("SP", [128, 4, 512], F32)
    TAB = sb("TAB", [128, 18, 512], F16)
    RA = sb("RA", [128, 4, 512], F32)
    R0, R1, A0, A1 = RA[:, 0, :], RA[:, 1, :], RA[:, 2, :], RA[:, 3, :]
    LNG = RA[:, 0:2, :].rearrange("p a b -> p (a b)")
    LNB = RA[:, 2:4, :].rearrange("p a b -> p (a b)")
    KVO = sb("KVO", [128, 2, 256], F32)
    FRAW = sb("FRAW", [128, max(NTL, 17), 2], F32)
    LFT = sb("LFT", [128, max(NTL, 17), 2], F32)
    CUMW = sb("CUMW", [128, max(NTL, 17) * 2], F32)
    NCUM = sb("NCUM", [128, max(NTL, 17) * 2 + 2 * NSL + 2], F32)
    CQ = sb("CQ", [128, 2, 512], F32)
    QM = sb("QM", [128, 2, 2, 512], BF16)
    DG = sb("DG", [128, 2, 128], F32)
    TOTB = sb("TOTB", [128, 128], F32)
    L2 = sb("L2", [128, 128 + 2 * NSL], F32)
    IDN = sb("IDN", [128, 128], F32)
    TRI = sb("TRI", [128, 128], F32)
    ONF = sb("ONF", [128, 128], F32)
    EPSC = sb("EPSC", [128, 2], F32)
    ONB = sb("ONB", [128, 128], BF16)
    MOD = sb("MOD", [128, 64], F32)
    GBC = sb("GBC", [128, D], F32)
    CC = sb("CC", [128, 16], F32)
    if SO >= 3 * D:
        M2 = UT[0:2, 0:2, :].rearrange("p a b -> p (a b)")[:, 0:6 * D].bitcast(F32)
    else:
        M2 = sb("M2", [2, 3 * D], F32)
    SEL = sb("SEL", [2, 130], F32)
    BFB = sb("BFB", [128, 8], F32)
    LAMR = sb("LAMR", [1, 256], F32)
    LAMC = sb("LAMC", [128, 4], F32)
    LFO = sb("LFO", [64, 256], F32)
    HTS = sb("HTS", [128, 8, TS], BF16)
    QTS = sb("QTS", [128, TS], BF16)
    GTS = sb("GTS", [128, TS], BF16)
    UTS = sb("UTS", [128, 8, TS], BF16)
    CLT = sb("CLT", [128, 16, 8], F32)
    STB = sb("STB", [128, 3, TS], F16)
    L2S = sb("L2S", [128, 36], F32)
    STAT = sb("STAT", [128, 16], F32)
    BNS = sb("BNS", [128, 4, 6], F32)

    banks = [es.enter_context(nc.psum_tensor("bank%d" % i, [128, 512], F32)) for i in range(8)]
    BK = ["b%d" % i for i in range(8)]

    P.pool(I("memset", IDN[:], 1.0), writes=["IDN"])
    P.pool(I("affine_select", out=IDN[:], in_=IDN[:], pattern=[[1, 128]], compare_op=ALU.is_equal,
                                     fill=0.0, base=0, channel_multiplier=-1), reads=["IDN"], writes=["IDN"])
    P.pool(I("memset", TRI[:], 1.0), writes=["TRI"])
    P.pool(I("affine_select", out=TRI[:], in_=TRI[:], pattern=[[1, 128]], compare_op=ALU.is_ge,
                                     fill=0.0, base=0, channel_multiplier=-1), reads=["TRI"], writes=["TRI"])
    P.pool(I("memset", ONF[:], 1.0), writes=["ONF"])
    P.pool(I("memset", EPSC[:], 1e-5), writes=["EPSC"])
    P.pool(I("memset", ONB[:], 1.0), writes=["ONB"])
    P.pool(I("memset", SEL[:], 0.0), writes=["SEL"])
    P.pool(I("memset", FRAW[:], 0.0), writes=["FRAW"])
    P.pool(I("memset", QM[:].rearrange("p a b c -> p (a b c)"), 0.0), writes=["QM0", "QM1"])
    P.pool(I("memset", VA[:, 16, :], 0.0), writes=["VAs0"])
    P.pool(I("memset", VA[:, 33, :], 0.0), writes=["VAs1"])
    if _os.environ.get('KDBG', '0') == '1' and SO < 3 * D:
        for _ph in range(8):
            P.pool(I("memset", UT[:, _ph, :], 0.0), writes=["UT"])
    P.dma("sp", I("dma_start", out=CC[:], in_=cc[:, :]), "c0", writes=["CC"])
    P.dma("sp", I("dma_start", out=M2[:], in_=b_ada2[:, :]), "c1", writes=["M2", "UT"])
    P.dma("sp", I("dma_start", out=BFB[:], in_=bf_bc[:, :]), "c2", writes=["BFB"])
    P.dma("sp", I("dma_start", out=LAMR[:], in_=lamv[:, :]), "c3", writes=["LAMR"])
    P.dma("sp", I("dma_start", out=LAMC[:, 1:2], in_=sg_col[:, :]), "c4", writes=["LAMC1"])
    P.dve(I("tensor_scalar", out=SEL[:, 0:129], in0=SEL[:, 0:129], scalar1=IDN[0:2, 0:1], scalar2=None, op0=ALU.add),
          reads=["SEL", "IDN"], writes=["SEL"])
    P.dve(I("tensor_scalar", out=SEL[:, 129:130], in0=SEL[:, 129:130], scalar1=IDN[0:2, 1:2], scalar2=None,
                                    op0=ALU.add), reads=["SEL", "IDN"], writes=["SEL"])
    SEL1 = sb("SEL1", [2, 128], F32)
    P.pool(I("memset", SEL1[:], 0.0), writes=["SEL1"])
    P.dve(I("tensor_scalar", out=SEL1[:], in0=SEL1[:], scalar1=IDN[0:2, 1:2], scalar2=None, op0=ALU.add),
          reads=["SEL1", "IDN"], writes=["SEL1"])

    SC = sb("SC", [128, 16], F32)
    P.act(I("activation", out=SC[:], in_=CC[:], func=AF.Silu), reads=["CC"], writes=["SC"])
    SC2 = sb("SC2", [128, 8, 2], F32)
    P.dve(I("tensor_copy", out=SC2[:, :, 0], in_=SC[:, 0:8]), reads=["SC"], writes=["SC2"])
    P.dve(I("tensor_copy", out=SC2[:, :, 1], in_=SC[:, 8:16]), reads=["SC"], writes=["SC2"])
    WA = [SP[:, 0:2, :], SP[:, 2:4, :]]
    wa_v = w_ada.rearrange("(kt p) n -> p kt n", p=128)
    n_ld = 0
    for ch in range(3):
        for kt in range(8):
            buf = n_ld % 2
            n_ld += 1
            P.dma("sp", I("dma_start",
                out=WA[buf].rearrange("p a b -> p (a b)"), in_=wa_v[:, kt, ch * 1024:(ch + 1) * 1024]),
                "wa%d" % buf, writes=["SP%d" % (2 * buf), "SP%d" % (2 * buf + 1)])
            for hf in range(2):
                P.pe(I("matmul",
                    banks[hf][0:2, :], lhsT=SC2[:, kt, :], rhs=WA[buf][:, hf, :], start=(kt == 0), stop=(kt == 7)),
                    reads=["SC2", "SP%d" % (2 * buf), "SP%d" % (2 * buf + 1)], writes=[BK[hf]])
        for hf in range(2):
            c0 = ch * 1024 + hf * 512
            P.dve(I("tensor_tensor", out=M2[:, c0:c0 + 512], in0=banks[hf][0:2, :],
                                                           in1=M2[:, c0:c0 + 512], op=ALU.add),
                  reads=[BK[hf], "M2"], writes=["M2", "UT"])
    for r in range(2):
        for kind in range(2):
            for kt in range(8):
                col = r * 16 + (0 if kind == 1 else 8) + kt
                src0 = kind * 1024 + kt * 128
                P.pe(I("matmul",
                    banks[2][:, col:col + 1], lhsT=M2[:, src0:src0 + 128], rhs=SEL[:, 128 + r:129 + r],
                    start=True, stop=True), reads=["M2", "UT", "SEL"], writes=[BK[2]])
    P.dve(I("tensor_copy", out=MOD[:, 0:32], in_=banks[2][:, 0:32]), reads=[BK[2]], writes=["MOD"])
    for r in range(2):
        P.dve(I("tensor_scalar", out=MOD[:, 16 * r:16 * r + 8], in0=MOD[:, 16 * r:16 * r + 8],
                                             scalar1=1.0, scalar2=None, op0=ALU.add), reads=["MOD"], writes=["MOD"])

    def load_gate_bc(r):
        selr = SEL[:, 0:128] if r == 0 else SEL1[:, :]
        for hf in range(2):
            P.pe(I("matmul", banks[hf][:, :], lhsT=selr, rhs=M2[:, 2048 + hf * 512:2048 + (hf + 1) * 512],
                                           start=True, stop=True), reads=["M2", "UT", "SEL", "SEL1"], writes=[BK[hf]])
            P.dve(I("tensor_copy", out=GBC[:, hf * 512:(hf + 1) * 512], in_=banks[hf][:, :]),
                  reads=[BK[hf]], writes=["GBC"])

    P.dma("sp", I("dma_start", out=L2[:], in_=l2[:, :]), "c5", writes=["L2"])
    P.dma("sp", I("dma_start", out=L2S[:], in_=l2s[:, :]), "c6", writes=["L2S"])
    P.dma("sp", I("dma_start", out=TAB[:, 0:8, :], in_=maskf[:, :, :]), "c7", writes=["TAB"])
    P.dma("sp", I("dma_start", out=TAB[:, 8:16, :], in_=nega[:, :, :]), "c8", pwrites=["TAB"])
    P.dma("sp", I("dma_start", out=TAB[:, 16:18, :], in_=negi[:, :, :]), "c9", pwrites=["TAB"])
    P.dma("sp", I("dma_start", out=STB[:, 0, :], in_=smask[:, :]), "c10", writes=["STB"])
    P.dma("sp", I("dma_start", out=STB[:, 1, :], in_=snega[:, :]), "c11", pwrites=["STB"])
    P.dma("sp", I("dma_start", out=STB[:, 2, :], in_=snegi[:, :]), "c12", pwrites=["STB"])
    LT = sb("LT", [1, 8], F32)
    P.dve(I("tensor_tensor", out=LAMR[:, 0:64], in0=LAMR[:, 0:64], in1=LAMR[:, 64:128], op=ALU.mult),
          reads=["LAMR"], writes=["LAMR"])
    P.dve(I("tensor_tensor", out=LAMR[:, 128:192], in0=LAMR[:, 128:192], in1=LAMR[:, 192:256], op=ALU.mult),
          reads=["LAMR"], writes=["LAMR"])
    P.dve(I("reduce_sum", out=LT[:, 0:1], in_=LAMR[:, 0:64], axis=mybir.AxisListType.X), reads=["LAMR"], writes=["LT"])
    P.dve(I("reduce_sum", out=LT[:, 1:2], in_=LAMR[:, 128:192], axis=mybir.AxisListType.X), reads=["LAMR"], writes=["LT"])
    P.act(I("activation", out=LT[:, 2:4], in_=LT[:, 0:2], func=AF.Exp), reads=["LT"], writes=["LT"])
    P.dve(I("scalar_tensor_tensor", out=LT[:, 4:5], in0=LT[:, 3:4], scalar=-LAM_INIT, in1=LT[:, 2:3],
                                           op0=ALU.add, op1=ALU.subtract), reads=["LT"], writes=["LT"])
    P.pe(I("matmul", banks[3][:, 0:1], lhsT=ONF[0:1, :], rhs=LT[:, 4:5], start=True, stop=True),
         reads=["ONF", "LT"], writes=[BK[3]])
    P.dve(I("tensor_copy", out=LAMC[:, 0:1], in_=banks[3][:, 0:1]), reads=[BK[3]], writes=["LAMC0"])
    P.dve(I("tensor_scalar", out=LAMC[:, 1:2], in0=LAMC[:, 1:2], scalar1=1.0 - LAM_INIT, scalar2=None, op0=ALU.mult),
          reads=["LAMC1"], writes=["LAMC1"])

    win_v = w_in.rearrange("(kt p) c -> p kt c", p=128)

    def phase_cols(ph):
        if ph < 4:
            return dict(k=512 + 128 * ph, v=1024 + 128 * ph, f=1536 + 2 * ph, q=128 * ph, g=1544 + 128 * ph)
        dh = ph - 4
        return dict(k=2568 + 128 * dh, v=3080 + 128 * dh, f=None, q=2056 + 128 * dh, g=3592 + 128 * dh)

    def load_win(ph):
        c = phase_cols(ph)
        lay = [(c["k"], 0, 128), (c["v"], 128, 128), (c["q"], 260, 128), (c["g"], 388, 128)]
        if c["f"] is not None:
            lay.append((c["f"], 256, 2))
        for i, (src, dst, n) in enumerate(lay):
            P.dma("pool", I("dma_start", out=WIN[:, :, dst:dst + n],
                                                                        in_=win_v[:, :, src:src + n]),
                  "win%d" % i, writes=["WINc%d" % (i % 2)], pwrites=["WIN"])

    cnt = {"fm": 0, "tm": 0, "ev": 0}

    def evac_copy(dst, src, reads, writes, scale=None, func=None):
        cnt["ev"] += 1
        if func is not None or (cnt["ev"] % 2 == 0):
            f = func if func is not None else AF.Copy
            if scale is None:
                P.act(I("activation", out=dst, in_=src, func=f), reads, pwrites=writes)
            else:
                P.act(I("activation", out=dst, in_=src, func=f, scale=scale), reads, pwrites=writes)
        else:
            if scale is None:
                P.dve(I("tensor_copy", out=dst, in_=src), reads, pwrites=writes)
            else:
                P.dve(I("tensor_scalar", out=dst, in0=src, scalar1=scale, scalar2=None, op0=ALU.mult), reads, pwrites=writes)

    def transpose_tokens(xsrc_tok, ntok, hdst_fn, htok, modcol, xtok, bankpair):
        for kt in range(8):
            bk = bankpair[kt // 4]
            P.pe(I("transpose", out=banks[bk][:, (kt % 4) * 128:(kt % 4) * 128 + ntok],
                                                     in_=xsrc_tok[:, kt * 128:(kt + 1) * 128],
                                                     identity=IDN[0:ntok, 0:ntok]),
                 reads=[xtok, "IDN"], writes=[BK[bk]])
        for kt in range(8):
            bk = bankpair[kt // 4]
            src = banks[bk][:, (kt % 4) * 128:(kt % 4) * 128 + ntok]
            dst = hdst_fn(kt)
            s1 = MOD[:, modcol + kt:modcol + kt + 1]
            sh = MOD[:, modcol + 8 + kt:modcol + 9 + kt]
            if kt < 4:
                P.act(I("activation", out=dst, in_=src, func=AF.Identity,
                                                                             bias=sh, scale=s1),
                      reads=[BK[bk], "MOD"], pwrites=[htok])
            else:
                P.dve(I("tensor_scalar", out=dst, in0=src, scalar1=s1,
                                                                                scalar2=sh, op0=ALU.mult, op1=ALU.add),
                      reads=[BK[bk], "MOD"], pwrites=[htok])

    def feat_proj(col0, rhs_fn, n, dst, reads, writes, scale=None, func=None):
        bk = 4 + (cnt["fm"] % 2)
        cnt["fm"] += 1
        for kt in range(8):
            P.pe(I("matmul", banks[bk][:, 0:n], lhsT=WIN[:, kt, col0:col0 + 128], rhs=rhs_fn(kt),
                                                  start=(kt == 0), stop=(kt == 7)),
                 reads=["WIN"] + list(reads), writes=[BK[bk]])
        evac_copy(dst, banks[bk][:, 0:n], [BK[bk]], writes, scale=scale, func=func)

    def tok_proj(lhs_fn, ntok, c0, c1, reads):
        bk = 6 + (cnt["tm"] % 2)
        cnt["tm"] += 1
        for kt in range(8):
            P.pe(I("matmul", banks[bk][0:ntok, 0:c1 - c0], lhsT=lhs_fn(kt), rhs=WIN[:, kt, c0:c1],
                                                  start=(kt == 0), stop=(kt == 7)),
                 reads=["WIN"] + list(reads), writes=[BK[bk]])
        return bk

    def logf_and_cum(ph, ntl, l2t, ncols_ref, l2tok):
        n2 = 2 * ntl
        fr = FRAW[:, 0:ntl, :].rearrange("p a b -> p (a b)")
        lf = LFT[:, 0:ntl, :].rearrange("p a b -> p (a b)")
        P.act(I("activation", out=lf, in_=fr, func=AF.Exp, scale=-1.0), reads=["FRAW"], writes=["LFT"])
        P.act(I("activation", out=lf, in_=lf, func=AF.Ln, bias=1.0), reads=["LFT"], writes=["LFT"])
        P.dve(I("tensor_scalar", out=lf, in0=lf, scalar1=-1.0, scalar2=None, op0=ALU.mult), reads=["LFT"], writes=["LFT"])
        return n2, lf

    def cum_from_lft(ntl, l2t, nref, l2tok, valid_last=None):
        n2 = 2 * ntl
        lf = LFT[:, 0:ntl, :].rearrange("p a b -> p (a b)")
        P.pe(I("matmul", banks[0][:, 0:n2], lhsT=TRI[:], rhs=lf, start=True, stop=True),
             reads=["TRI", "LFT"], writes=[BK[0]])
        P.dve(I("tensor_copy", out=CUMW[:, 0:n2], in_=banks[0][:, 0:n2]), reads=[BK[0]], writes=["CUMW"])
        P.pe(I("matmul", banks[1][0:n2, 0:1], lhsT=lf, rhs=ONF[:, 0:1], start=True, stop=True),
             reads=["LFT", "ONF"], writes=[BK[1]])
        P.dve(I("tensor_scalar", out=TOTB[0:n2, :], in0=ONF[0:n2, :], scalar1=banks[1][0:n2, 0:1], scalar2=None,
                                        op0=ALU.mult), reads=[BK[1], "ONF"], writes=["TOTB"])
        ncol = n2 + nref
        P.pe(I("matmul", banks[1][:, 0:ncol], lhsT=TOTB[0:n2, :], rhs=l2t[0:n2, 0:ncol], start=True, stop=True),
             reads=["TOTB", l2tok], writes=[BK[1]])
        P.dve(I("scalar_tensor_tensor", out=NCUM[:, 0:n2], in0=banks[1][:, 0:n2], scalar=-1.0, in1=CUMW[:, 0:n2],
                                               op0=ALU.mult, op1=ALU.subtract), reads=[BK[1], "CUMW"], writes=["NCUM"])
        P.dve(I("tensor_copy", out=NCUM[:, n2:ncol], in_=banks[1][:, n2:ncol]), reads=[BK[1]], writes=["NCUM"])


    cqc = {"n": 0}

    def build_cq(cols_by_hh, n, bank_i):
        for hh in range(2):
            for a, col in enumerate(cols_by_hh[hh]):
                dg = cqc["n"] % 2
                cqc["n"] += 1
                P.act(I("activation", out=DG[0:n, dg, 0:n], in_=IDN[0:n, 0:n], func=AF.Copy, scale=NCUM[0:n, col:col + 1]),
                      reads=["IDN", "NCUM"], writes=["DG%d" % dg])
                P.pe(I("matmul", banks[bank_i][:, hh * 256 + a * n: hh * 256 + (a + 1) * n] if len(cols_by_hh[hh]) * n <= 256
                       else banks[bank_i + hh][:, a * n:(a + 1) * n],
                       lhsT=ONF[0:n, :], rhs=DG[0:n, dg, 0:n], start=True, stop=True),
                     reads=["ONF", "DG%d" % dg], writes=[BK[bank_i], BK[bank_i + 1]])
        w = len(cols_by_hh[0]) * n
        for hh in range(2):
            src = banks[bank_i][:, hh * 256: hh * 256 + w] if w <= 256 else banks[bank_i + hh][:, 0:w]
            P.act(I("activation", out=CQ[:, hh, 0:w], in_=src, func=AF.Copy, scale=-1.0),
                  reads=[BK[bank_i], BK[bank_i + 1]], pwrites=["CQ"])

    acnt = {"n": 0}

    def attention(kind, ph, nq, qT, tiles, gT, uT_dst, slope=None, qtok="QT", gtok="GT", uttok="UT", ktok="KT", vtok="VA",
                  stage="all", qb=None):
        nt = len(tiles)
        if stage in ("all", "prep"):
            qb = acnt["n"] % 2
            acnt["n"] += 1
            P.act(I("activation", out=QM[0:64, qb, 0, 0:nq], in_=qT[0:64, :], func=AF.Copy), reads=[qtok], pwrites=["QM%d" % qb])
            P.act(I("activation", out=QM[64:128, qb, 1, 0:nq], in_=qT[64:128, :], func=AF.Copy), reads=[qtok], pwrites=["QM%d" % qb])
            if stage == "prep":
                return qb
        qmk = "QM%d" % qb

        def score(h):
            st, c = h // 2, h % 2
            tl = tiles[st]
            nk = tl["nk"]
            sbk = h % 4
            P.pe(I("matmul", banks[sbk][0:nk, 0:nq], lhsT=tl["kT"], rhs=QM[:, qb, c, 0:nq], start=True, stop=True),
                 reads=[ktok, qmk], writes=[BK[sbk]])
            pt = PT[0:nk, sbk, 0:nq]
            ptk = "PT%d" % sbk
            sp = SP[0:nk, sbk, 0:nq]
            spk = "SP%d" % sbk
            sb_ap = banks[sbk][0:nk, 0:nq]
            if kind == "fox":
                if tl["mask"] is None:
                    P.dve(I("tensor_tensor", out=sb_ap, in0=sb_ap, in1=tl["cq"][c], op=ALU.add),
                          reads=[BK[sbk], "CQ"], writes=[BK[sbk]])
                    P.act(I("activation", out=pt, in_=sb_ap, func=AF.Exp, bias=tl["bias"][c]),
                          reads=[BK[sbk], "NCUM"], writes=[ptk])
                else:
                    P.dve(I("tensor_tensor", out=sp, in0=sb_ap, in1=tl["cq"][c], op=ALU.add),
                          reads=[BK[sbk], "CQ"], writes=[spk])
                    P.pool(I("tensor_tensor", out=sp, in0=sp, in1=tl["mask"], op=ALU.add),
                           reads=[spk, "TAB", "STB"], writes=[spk])
                    P.act(I("activation", out=pt, in_=sp, func=AF.Exp, bias=tl["bias"][c]),
                          reads=[spk, "NCUM"], writes=[ptk])
            else:
                P.dve(I("scalar_tensor_tensor", out=sb_ap, in0=tl["table"], scalar=slope, in1=sb_ap,
                        op0=ALU.mult, op1=ALU.add), reads=[BK[sbk], "TAB", "STB"], writes=[BK[sbk]])
                P.act(I("activation", out=pt, in_=sb_ap, func=AF.Exp, bias=float(tl["bias"])),
                      reads=[BK[sbk]], writes=[ptk])

        def pv(h):
            st, c = h // 2, h % 2
            tl = tiles[st]
            nk = tl["nk"]
            first, last = (st == 0), (st == nt - 1)
            pt = PT[0:nk, h % 4, 0:nq]
            ptk = "PT%d" % (h % 4)
            P.pe(I("matmul", banks[4 + c][:, 0:nq], lhsT=tl["v"], rhs=pt, start=first, stop=last),
                 reads=[vtok, ptk], writes=[BK[4 + c]])
            P.pe(I("matmul", banks[6 + c][:, 0:nq], lhsT=ONB[0:nk, :], rhs=pt, start=first, stop=last),
                 reads=["ONB", ptk], writes=[BK[6 + c]])

        if stage == "finish":
            pass
        elif nq * nt <= 512:
            W = nq * nt
            for c in range(2):
                sbk = c
                spk = "SP%d" % c
                for st, tl in enumerate(tiles):
                    nk = tl["nk"]
                    P.pe(I("matmul", banks[sbk][0:nk, st * nq:(st + 1) * nq], lhsT=tl["kT"], rhs=QM[:, qb, c, 0:nq],
                           start=True, stop=True), reads=[ktok, qmk], writes=[BK[sbk]])
                for st, tl in enumerate(tiles):
                    nk = tl["nk"]
                    dst = SP[0:nk, c, st * nq:(st + 1) * nq]
                    if kind == "fox":
                        P.dve(I("tensor_scalar", out=dst, in0=tl["cq"][c], scalar1=tl["bias"][c], scalar2=None, op0=ALU.add),
                              reads=["CQ", "NCUM"], pwrites=[spk])
                        if tl["mask"] is not None:
                            P.dve(I("tensor_tensor", out=dst, in0=dst, in1=tl["mask"], op=ALU.add),
                                  reads=[spk, "STB", "TAB"], writes=[spk])
                    else:
                        P.dve(I("tensor_scalar", out=dst, in0=tl["table"], scalar1=float(slope), scalar2=float(tl["bias"]),
                                op0=ALU.mult, op1=ALU.add), reads=["STB", "TAB"], pwrites=[spk])
                P.dve(I("tensor_tensor", out=banks[sbk][:, 0:W], in0=banks[sbk][:, 0:W], in1=SP[:, c, 0:W], op=ALU.add),
                      reads=[BK[sbk], spk], writes=[BK[sbk]])
                P.act(I("activation", out=PT[:, c, 0:W], in_=banks[sbk][:, 0:W], func=AF.Exp),
                      reads=[BK[sbk]], writes=["PT%d" % c])
            for c in range(2):
                for st, tl in enumerate(tiles):
                    nk = tl["nk"]
                    pt = PT[0:nk, c, st * nq:(st + 1) * nq]
                    P.pe(I("matmul", banks[4 + c][:, 0:nq], lhsT=tl["v"], rhs=pt, start=(st == 0), stop=(st == nt - 1)),
                         reads=[vtok, "PT%d" % c], writes=[BK[4 + c]])
                    P.pe(I("matmul", banks[6 + c][:, 0:nq], lhsT=ONB[0:nk, :], rhs=pt, start=(st == 0), stop=(st == nt - 1)),
                         reads=["ONB", "PT%d" % c], writes=[BK[6 + c]])
        else:
            LOOK = 3
            for h in range(2 * nt + LOOK):
                if h < 2 * nt:
                    score(h)
                if h >= LOOK:
                    pv(h - LOOK)
        if stage == "loop":
            return qb
        r0, r1, a0, a1 = R0[:, 0:nq], R1[:, 0:nq], A0[:, 0:nq], A1[:, 0:nq]
        if kind == "fox":
            for c in range(2):
                lo, hi = c * 64, (c + 1) * 64
                P.act(I("activation", out=R0[lo:hi, 0:nq], in_=banks[6 + c][lo:hi, 0:nq], func=AF.Ln), reads=[BK[6 + c]], pwrites=["R0"])
                P.act(I("activation", out=R0[lo:hi, 0:nq], in_=R0[lo:hi, 0:nq], func=AF.Exp, scale=-1.0), reads=["R0"], pwrites=["R0"])
            for c in range(2):
                lo, hi = c * 64, (c + 1) * 64
                P.dve(I("tensor_tensor", out=A0[lo:hi, 0:nq], in0=banks[4 + c][lo:hi, 0:nq], in1=R0[lo:hi, 0:nq], op=ALU.mult),
                      reads=[BK[4 + c], "R0"], pwrites=["A0"])
            P.pool(I("tensor_tensor", out=uT_dst, in0=a0, in1=gT, op=ALU.mult), reads=["A0", gtok], pwrites=[uttok])
        else:
            P.act(I("activation", out=r0, in_=banks[6][:, 0:nq], func=AF.Ln), reads=[BK[6]], writes=["R0"])
            P.act(I("activation", out=r1, in_=banks[7][:, 0:nq], func=AF.Ln), reads=[BK[7]], writes=["R1"])
            P.act(I("activation", out=r0, in_=r0, func=AF.Exp, scale=-1.0), reads=["R0"], writes=["R0"])
            P.act(I("activation", out=r1, in_=r1, func=AF.Exp, scale=-1.0), reads=["R1"], writes=["R1"])
            P.dve(I("tensor_tensor", out=a0, in0=banks[4][:, 0:nq], in1=r0, op=ALU.mult), reads=[BK[4], "R0"], writes=["A0"])
            P.dve(I("tensor_tensor", out=a1, in0=banks[5][:, 0:nq], in1=r1, op=ALU.mult), reads=[BK[5], "R1"], writes=["A1"])
            P.dve(I("scalar_tensor_tensor", out=a0, in0=a1, scalar=LAMC[:, 0:1], in1=a0, op0=ALU.mult, op1=ALU.add),
                  reads=["A0", "A1", "LAMC0"], writes=["A0"])
            P.act(I("activation", out=r0, in_=a0, func=AF.Square), reads=["A0"], writes=["R0"])
            P.pe(I("matmul", banks[0][:, 0:nq], lhsT=ONF[:], rhs=r0, start=True, stop=True),
                 reads=["ONF", "R0"], writes=[BK[0]])
            P.act(I("activation", out=r1, in_=banks[0][:, 0:nq], func=AF.Ln, scale=1.0 / 128.0, bias=EPSC[:, 0:1]),
                  reads=[BK[0], "EPSC"], writes=["R1"])
            P.act(I("activation", out=r1, in_=r1, func=AF.Exp, scale=-0.5), reads=["R1"], writes=["R1"])
            P.dve(I("tensor_tensor", out=a0, in0=a0, in1=r1, op=ALU.mult), reads=["A0", "R1"], writes=["A0"])
            P.act(I("activation", out=a0, in_=a0, func=AF.Copy, scale=LAMC[:, 1:2]),
                  reads=["A0", "LAMC1"], writes=["A0"])
            P.pool(I("tensor_tensor", out=uT_dst, in0=a0, in1=gT, op=ALU.mult), reads=["A0", gtok], pwrites=[uttok])

    def load_ln_tables():
        P.dma("sp", I("dma_start", out=LNG, in_=lng_bc[:, :]), "c14", writes=["R0", "R1"])
        P.dma("sp", I("dma_start", out=LNB, in_=lnb_bc[:, :]), "c15", writes=["A0", "A1"])

    def load_wout():
        wo_v = w_out.rearrange("(ph p) n -> p ph n", p=128)
        wdst = KT[:, 0:8 * D].rearrange("p (a b) -> p a b", a=8)
        for hf in range(2):
            P.dma("pool", I("dma_start", out=wdst[:, hf * 4:(hf + 1) * 4, :], in_=wo_v[:, hf * 4:(hf + 1) * 4, :]),
                  "wo%d" % hf, writes=(["KT"] if hf == 0 else []), pwrites=([] if hf == 0 else ["KT"]))
        for ph in range(8):
            eng = P.pool if ph % 2 == 0 else P.dve
            eng(I("tensor_tensor", out=wdst[:, ph, :], in0=wdst[:, ph, :], in1=GBC[:, :], op=ALU.mult),
                reads=["KT", "GBC"], pwrites=["KT"])
        return wdst

    opc = {"n": 0}

    def out_proj_ln(ntok, ut_fn, wdst, xtile, xtok, ydst_ap, ytok_sem, ybank, dq="sp"):
        par = opc["n"] % 2
        opc["n"] += 1
        for hf in range(2):
            bk = ybank + hf
            for ph in range(8):
                P.pe(I("matmul", banks[bk][0:ntok, :], lhsT=ut_fn(ph), rhs=wdst[:, ph, hf * 512:(hf + 1) * 512],
                       start=(ph == 0), stop=(ph == 7)), reads=["UT", "UTS", "KT"], writes=[BK[bk]])
        if par == 0:
            Y = SP[0:ntok, 0:2, :].rearrange("p a b -> p (a b)")
            Z = SP[0:ntok, 2:4, :].rearrange("p a b -> p (a b)")
            yt, zt = ["SP0", "SP1"], ["SP2", "SP3"]
        else:
            Y = CQ[0:ntok, :, :].rearrange("p a b -> p (a b)")
            Z = PT[0:ntok, :, :].rearrange("p a b -> p (a b)").bitcast(F32)
            yt, zt = ["CQ"], ["PT0", "PT1", "PT2", "PT3"]
        st0 = 4 * par
        bnt, stt = "BNS%d" % par, "STAT%d" % par
        for hf in range(2):
            bk = ybank + hf
            P.dve(I("scalar_tensor_tensor", out=Y[:, hf * 512:(hf + 1) * 512], in0=xtile[:, hf * 512:(hf + 1) * 512],
                    scalar=ALPHA, in1=banks[bk][0:ntok, :], op0=ALU.mult, op1=ALU.add),
                  reads=[xtok, BK[bk]], pwrites=yt)
        for hf in range(2):
            P.dve(I("bn_stats", out=BNS[0:ntok, 2 * par + hf, :], in_=Y[:, hf * 512:(hf + 1) * 512]),
                  reads=yt, pwrites=[bnt])
        P.dve(I("bn_aggr", out=STAT[0:ntok, st0:st0 + 2], in_=BNS[0:ntok, 2 * par:2 * par + 2, :].rearrange("p a b -> p (a b)")),
              reads=[bnt], writes=[stt])
        P.act(I("activation", out=STAT[0:ntok, st0 + 2:st0 + 3], in_=STAT[0:ntok, st0 + 1:st0 + 2], func=AF.Ln, bias=EPSC[0:ntok, 0:1]),
              reads=[stt, "EPSC"], writes=[stt])
        P.act(I("activation", out=STAT[0:ntok, st0 + 3:st0 + 4], in_=STAT[0:ntok, st0 + 2:st0 + 3], func=AF.Exp, scale=-0.5),
              reads=[stt], writes=[stt])
        P.dve(I("tensor_scalar", out=Y, in0=Y, scalar1=STAT[0:ntok, st0:st0 + 1], scalar2=STAT[0:ntok, st0 + 3:st0 + 4],
                op0=ALU.subtract, op1=ALU.mult), reads=[stt] + yt, writes=yt)
        P.pool(I("tensor_tensor", out=Z, in0=Y, in1=LNG[0:ntok, :], op=ALU.mult), reads=yt + ["R0", "R1"], writes=zt)
        P.pool(I("tensor_tensor", out=Z, in0=Z, in1=LNB[0:ntok, :], op=ALU.add), reads=zt + ["A0", "A1"], writes=zt)
        P.dma(dq, I("dma_start", out=ydst_ap, in_=Z), ytok_sem + str(par), reads=zt, pwrites=["OUT"])

    P.dma("sp", I("dma_start", out=XS[0:TS, 0, :], in_=xsm[:, :]), "xs0", writes=["XS0"])
    transpose_tokens(XS[0:TS, 0, :], TS, lambda kt: HTS[:, kt, :], "HTS", 16, "XS0", (0, 1))
    CLR = XS[0:8, 0:2, :].rearrange("p a b -> p (a b)")
    P.dma("sp", I("dma_start", out=CLR, in_=cfl[:, :]), "c13", writes=["XS0", "XS1"])
    for tl in range(16):
        P.pe(I("transpose", out=banks[2][:, tl * 8:(tl + 1) * 8], in_=CLR[:, tl * 128:(tl + 1) * 128],
                                          identity=IDN[0:8, 0:8]), reads=["XS0", "XS1", "IDN"], writes=[BK[2]])
    P.dve(I("tensor_copy", out=CLT[:].rearrange("p a b -> p (a b)"), in_=banks[2][:, 0:128]), reads=[BK[2]], writes=["CLT"])

    import os as _os
    _sph = [int(v) for v in _os.environ.get('KSPH', '0,1,2,3,4,5,6,7').split(',') if v != '']
    KSTR = 17 * 128

    def s_load_k(ph):
        CK = XS[:, 0:2, :].rearrange("p a b -> p (a b)")
        if ph < 4:
            ckv2 = CK.rearrange("p (t h d) -> p t h d", t=16, h=2)
            for hh in range(2):
                P.dma("sp", I("dma_start", out=ckv2[:, :, hh, :],
                              in_=cfk[2 * ph + hh, :, :].rearrange("(t p) d -> p t d", p=128)),
                      "ck%d_%d" % (hh, ph % 2), writes=["XS0", "XS1"] if hh == 0 else [], pwrites=[] if hh == 0 else ["XS0", "XS1"])
        else:
            ckv = CK.rearrange("p (t d) -> p t d", t=16)
            P.dma("sp", I("dma_start", out=ckv, in_=cdk[ph - 4, :, :].rearrange("(t p) d -> p t d", p=128)),
                  "ck0_%d" % (ph % 2), writes=["XS0", "XS1"])

    def s_load_v(ph):
        vo = 17 * (ph % 2)
        vtok = "VAs%d" % (ph % 2)
        if ph < 4:
            for hh in range(2):
                P.dma("pool", I("dma_start", out=VA[:, vo:vo + 16, hh * 64:(hh + 1) * 64],
                                in_=cfv[2 * ph + hh, :, :].rearrange("(t p) d -> p t d", p=128)),
                      "cv%d_%d" % (hh, ph % 2), pwrites=[vtok])
        else:
            P.dma("pool", I("dma_start", out=VA[:, vo:vo + 16, :], in_=cdv[ph - 4, :, :].rearrange("(t p) d -> p t d", p=128)),
                  "cv0_%d" % (ph % 2), pwrites=[vtok])

    def s_kt(ph):
        ko = KSTR * (ph % 2)
        ktok = "KTs%d" % (ph % 2)
        ckv = XS[:, 0:2, :].rearrange("p a b -> p (a b)").rearrange("p (t e) -> p t e", t=16)
        for g4 in range(4):
            bk = g4 % 2
            for i4 in range(4):
                tl = g4 * 4 + i4
                P.pe(I("transpose", out=banks[bk][:, i4 * 128:(i4 + 1) * 128], in_=ckv[:, tl, :], identity=IDN[:]),
                     reads=["XS0", "XS1", "IDN"], writes=[BK[bk]])
            evac_copy(KT[:, ko + g4 * 512:ko + (g4 + 1) * 512], banks[bk][:, :], [BK[bk]], [ktok])

    def s_proj(ph):
        kind = "fox" if ph < 4 else "diff"
        ko, vo = KSTR * (ph % 2), 17 * (ph % 2)
        ktok, vtok = "KTs%d" % (ph % 2), "VAs%d" % (ph % 2)
        feat_proj(0, lambda kt: HTS[:, kt, :], TS, KT[:, ko + PAST:ko + PAST + TS], ["HTS"], [ktok])
        feat_proj(260, lambda kt: HTS[:, kt, :], TS, QTS[:, :], ["HTS"], ["QTS"], scale=0.125)
        feat_proj(388, lambda kt: HTS[:, kt, :], TS, GTS[:, :], ["HTS"], ["GTS"], func=AF.Silu)
        ncol = 258 if kind == "fox" else 256
        bk = tok_proj(lambda kt: HTS[:, kt, :], TS, 0, ncol, ["HTS"])
        P.dve(I("tensor_copy", out=VA[0:TS, vo + 16, :], in_=banks[bk][0:TS, 128:256]), reads=[BK[bk]], pwrites=[vtok])
        P.dve(I("tensor_copy", out=KVO[0:TS, 0, :], in_=banks[bk][0:TS, 0:256]), reads=[BK[bk]], writes=["KVO0"])
        if kind == "fox":
            P.dma("sp", I("dma_start", out=sfk_o[2 * ph:2 * ph + 2, :, :].rearrange("h t d -> t h d"),
                          in_=KVO[0:TS, 0, 0:128].rearrange("p (h d) -> p h d", h=2)), "so0", reads=["KVO0"], pwrites=["OUT"])
            P.dma("sp", I("dma_start", out=sfv_o[2 * ph:2 * ph + 2, :, :].rearrange("h t d -> t h d"),
                          in_=KVO[0:TS, 0, 128:256].rearrange("p (h d) -> p h d", h=2)), "so1", reads=["KVO0"], pwrites=["OUT"])
            P.dve(I("tensor_tensor", out=FRAW[0:TS, 16, :], in0=banks[bk][0:TS, 256:258],
                    in1=BFB[0:TS, 2 * ph:2 * ph + 2], op=ALU.add), reads=[BK[bk], "BFB"], pwrites=["FRAW"])
        else:
            P.dma("sp", I("dma_start", out=sdk_o[ph - 4, :, :], in_=KVO[0:TS, 0, 0:128]), "so0", reads=["KVO0"], pwrites=["OUT"])
            P.dma("sp", I("dma_start", out=sdv_o[ph - 4, :, :], in_=KVO[0:TS, 0, 128:256]), "so1", reads=["KVO0"], pwrites=["OUT"])

    def s_attn(ph):
        kind = "fox" if ph < 4 else "diff"
        ko, vo = KSTR * (ph % 2), 17 * (ph % 2)
        ktok, vtok = "KTs%d" % (ph % 2), "VAs%d" % (ph % 2)
        if kind == "fox":
            fr = FRAW[0:TS, 16, :]
            lfn = LFT[0:TS, 16, :]
            P.pool(I("memset", LFT[:, 16, :], 0.0), writes=["LFT"])
            P.act(I("activation", out=lfn, in_=fr, func=AF.Exp, scale=-1.0), reads=["FRAW"], writes=["LFT"])
            P.act(I("activation", out=lfn, in_=lfn, func=AF.Ln, bias=1.0), reads=["LFT"], writes=["LFT"])
            P.dve(I("tensor_scalar", out=lfn, in0=lfn, scalar1=-1.0, scalar2=None, op0=ALU.mult), reads=["LFT"], writes=["LFT"])
            P.dve(I("tensor_copy", out=LFT[:, 0:16, :], in_=CLT[:, :, 2 * ph:2 * ph + 2]), reads=["CLT"], writes=["LFT"])
            P.dma("sp", I("dma_start", out=sfl_o[2 * ph:2 * ph + 2, :].rearrange("h t -> t h"), in_=LFT[0:TS, 16, :],
                          allow_slow_non_contiguous=True), "so2", reads=["LFT"], pwrites=["OUT"])
            cum_from_lft(17, L2S, 2, "L2S")
            build_cq([[32], [33]], TS, 0)
        tiles = []
        for tl in range(17):
            nk = 128 if tl < 16 else TS
            d = dict(kT=KT[:, ko + tl * 128:ko + tl * 128 + nk], v=VA[0:nk, vo + tl, :], nk=nk, mask=None, table=None)
            if kind == "fox":
                d["bias"] = [NCUM[0:nk, 2 * tl:2 * tl + 1], NCUM[0:nk, 2 * tl + 1:2 * tl + 2]]
                d["cq"] = [CQ[0:nk, 0, 0:TS], CQ[0:nk, 1, 0:TS]]
                if tl == 16:
                    d["mask"] = STB[0:TS, 0, :]
            else:
                if tl < 16:
                    d["table"] = STB[:, 2, :]
                    d["bias"] = -SLOPES[ph - 4] * (PAST - 128 * tl)
                else:
                    d["table"] = STB[0:TS, 1, :]
                    d["bias"] = 0.0
            tiles.append(d)
        attention(kind, ph, TS, QTS[:, :], tiles, GTS[:, :], UTS[:, ph, :], slope=(SLOPES[ph - 4] if ph >= 4 else None),
                  qtok="QTS", gtok="GTS", uttok="UTS", ktok=ktok, vtok=vtok)

    if "sample" in parts:
        load_win(0)
        s_load_k(0)
        s_load_v(0)
        for ph in range(8):
            s_kt(ph)
            if ph + 1 < 8:
                s_load_k(ph + 1)
                s_load_v(ph + 1)
            s_proj(ph)
            if ph + 1 < 8:
                load_win(ph + 1)
            s_attn(ph)
        P.dve(I("memset", STAT[:, 15:16], 0.0), reads=["KTs0", "KTs1", "VAs0", "VAs1"], writes=["KT", "VA"])

    load_gate_bc(1)
    wdst = load_wout()
    load_ln_tables()
    P.dma("sp", I("dma_start", out=XS[0:TS, 0, :], in_=xsm[:, :]), "xs0", writes=["XS0"])
    out_proj_ln(TS, lambda ph: UTS[:, ph, :], wdst, XS[0:TS, 0, :], "XS0", ys_o[:, :], "ys", 0)

    load_gate_bc(0)
    NSTEP = S // 256
    _pph = [int(v) for v in _os.environ.get('KPPH', '0,1,2,3,4,5,6,7').split(',') if v != '']

    def kv_prefetch(ph_):
        load_win(ph_)
        for n0 in range(2):
            P.dma("sp", I("dma_start", out=XS[:, n0, :], in_=xp[n0 * 128:(n0 + 1) * 128, :]), "xs%d" % n0, writes=["XS%d" % n0])

    for ph in (_pph if "prompt" in parts else []):
        kind = "fox" if ph < 4 else "diff"
        if ph == _pph[0]:
            kv_prefetch(ph)
        ncol = 258 if kind == "fox" else 256

        def kv_T(step):
            hb = step % 2
            htok = "HT%d" % hb
            for sub in range(2):
                ns_ = 2 * step + sub
                xb = ns_ % 3
                nxt = ns_ + 2
                if nxt < 2 * NSTEP:
                    P.dma("sp", I("dma_start", out=XS[:, nxt % 3, :], in_=xp[nxt * 128:(nxt + 1) * 128, :]),
                          "xs%d" % (nxt % 3), writes=["XS%d" % (nxt % 3)])
                bp = ns_ % 2
                transpose_tokens(XS[:, xb, :], 128, lambda kt, hb=hb, sub=sub: HT[:, hb, kt, sub * 128:(sub + 1) * 128],
                                 htok, 0, "XS%d" % xb, (2 * bp, 2 * bp + 1))

        def kv_proj(step):
            own = step < NSTEP // 2
            hb = step % 2
            htok = "HT%d" % hb
            tok0 = step * 256
            feat_proj(0, lambda kt, hb=hb: HT[:, hb, kt, :], 256, KT[:, tok0:tok0 + 256], [htok], ["KT"])
            if own:
                feat_proj(260, lambda kt, hb=hb: HT[:, hb, kt, :], 256, QT[:, tok0:tok0 + 256], [htok], ["QT"], scale=0.125)
                feat_proj(388, lambda kt, hb=hb: HT[:, hb, kt, :], 256, GT[:, tok0:tok0 + 256], [htok], ["GT"], func=AF.Silu)
            for sub in range(2):
                tile_i = step * 2 + sub
                t0 = tok0 + sub * 128
                c0 = 0 if own else 128
                bk = tok_proj(lambda kt, hb=hb, sub=sub: HT[:, hb, kt, sub * 128:(sub + 1) * 128], 128, c0, ncol, [htok])
                voff = 128 - c0
                P.dve(I("tensor_copy", out=VA[:, tile_i, :], in_=banks[bk][:, voff:voff + 128]),
                      reads=[BK[bk]], pwrites=["VA"])
                if kind == "fox":
                    P.dve(I("tensor_tensor", out=FRAW[:, tile_i, :], in0=banks[bk][:, voff + 128:voff + 130],
                            in1=BFB[:, 2 * ph:2 * ph + 2], op=ALU.add), reads=[BK[bk], "BFB"], pwrites=["FRAW"])
                if own:
                    kb = tile_i % 2
                    kvt = "KVO%d" % kb
                    P.dve(I("tensor_copy", out=KVO[:, kb, :], in_=banks[bk][:, 0:256]), reads=[BK[bk]], writes=[kvt])
                    if kind == "fox":
                        P.dma("pool", I("dma_start", out=fk_o[2 * ph:2 * ph + 2, t0:t0 + 128, :].rearrange("h t d -> t h d"),
                                        in_=KVO[:, kb, 0:128].rearrange("p (h d) -> p h d", h=2)), "ko%d" % kb, reads=[kvt], pwrites=["OUT"])
                        P.dma("pool", I("dma_start", out=fv_o[2 * ph:2 * ph + 2, t0:t0 + 128, :].rearrange("h t d -> t h d"),
                                        in_=KVO[:, kb, 128:256].rearrange("p (h d) -> p h d", h=2)), "vo%d" % kb, reads=[kvt], pwrites=["OUT"])
                    else:
                        P.dma("pool", I("dma_start", out=dk_o[ph - 4, t0:t0 + 128, :], in_=KVO[:, kb, 0:128]),
                              "ko%d" % kb, reads=[kvt], pwrites=["OUT"])
                        P.dma("pool", I("dma_start", out=dv_o[ph - 4, t0:t0 + 128, :], in_=KVO[:, kb, 128:256]),
                              "vo%d" % kb, reads=[kvt], pwrites=["OUT"])

        kv_T(0)
        for step in range(NSTEP):
            if step + 1 < NSTEP:
                kv_T(step + 1)
            kv_proj(step)
        if kind == "fox":
            logf_and_cum(ph, NTL, L2, 2 * NSL, "L2")
            cum_from_lft(NTL, L2, 2 * NSL, "L2")
            for hh in range(2):
                P.pe(I("transpose", out=banks[2][0:NTO, hh * 128:(hh + 1) * 128], in_=LFT[:, 0:NTO, hh], identity=IDN[:]),
                     reads=["LFT", "IDN"], writes=[BK[2]])
            P.dve(I("tensor_copy", out=LFO[0:NTO, :], in_=banks[2][0:NTO, 0:256]), reads=[BK[2]], writes=["LFO"])
            for hh in range(2):
                P.dma("sp", I("dma_start", out=fl_o[2 * ph + hh, :].rearrange("(t p) -> t p", p=128),
                              in_=LFO[0:NTO, hh * 128:(hh + 1) * 128]), "lfo%d" % hh, reads=["LFO"], pwrites=["OUT"])
        _nx = _pph.index(ph) + 1
        if _nx < len(_pph):
            kv_prefetch(_pph[_nx])
        slope_ = (SLOPES[ph - 4] if ph >= 4 else None)

        def slot_tiles(i):
            tiles = []
            past = [to for to in range(4 * i)] + [NTO + to for to in range(4 * i)]
            zone = [4 * i + z for z in range(4)] + [NTO + 4 * i + z for z in range(4)]
            for tl in past:
                d = dict(kT=KT[:, tl * 128:(tl + 1) * 128], v=VA[:, tl, :], nk=128, mask=None, table=None)
                if kind == "fox":
                    d["bias"] = [NCUM[:, 2 * tl:2 * tl + 1], NCUM[:, 2 * tl + 1:2 * tl + 2]]
                    d["cq"] = [CQ[:, 0, :], CQ[:, 1, :]]
                else:
                    to = tl % NTO
                    d["table"] = TAB[:, 16 + (0 if tl < NTO else 1), :]
                    d["bias"] = -SLOPES[ph - 4] * (1024 * i - 512 * (to // 2) - 128 * (to % 2))
                tiles.append(d)
            for z, tl in enumerate(zone):
                d = dict(kT=KT[:, tl * 128:(tl + 1) * 128], v=VA[:, tl, :], nk=128, mask=None, table=None)
                if kind == "fox":
                    d["bias"] = [NCUM[:, 2 * tl:2 * tl + 1], NCUM[:, 2 * tl + 1:2 * tl + 2]]
                    d["cq"] = [CQ[:, 0, :], CQ[:, 1, :]]
                    d["mask"] = TAB[:, z, :]
                else:
                    d["table"] = TAB[:, 8 + z, :]
                    d["bias"] = 0.0
                tiles.append(d)
            return tiles

        def slot_prep(i):
            if kind == "fox":
                build_cq([[(4 * i + a_) * 2 + hh_ for a_ in range(4)] for hh_ in range(2)], 128, 0)
            return attention(kind, ph, 512, QT[:, i * 512:(i + 1) * 512], [], None, None, stage="prep")

        qb_ = slot_prep(0)
        for i in range(NSL):
            q0 = i * 512
            tiles = slot_tiles(i)
            attention(kind, ph, 512, QT[:, q0:q0 + 512], tiles, GT[:, q0:q0 + 512], UT[:, ph, q0:q0 + 512],
                      slope=slope_, stage="loop", qb=qb_)
            qb_next = slot_prep(i + 1) if i + 1 < NSL else None
            attention(kind, ph, 512, QT[:, q0:q0 + 512], tiles, GT[:, q0:q0 + 512], UT[:, ph, q0:q0 + 512],
                      slope=slope_, stage="finish", qb=qb_)
            qb_ = qb_next

    wdst = load_wout()
    load_ln_tables()
    _fin = (NTO if "prompt" in parts else 0)
    for t0_ in range(min(2, _fin)):
        P.dma("sp", I("dma_start", out=XS[:, t0_ % 3, :], in_=xp[t0_ * 128:(t0_ + 1) * 128, :]),
              "xs%d" % (t0_ % 3), writes=["XS%d" % (t0_ % 3)])
    for tt in range(_fin):
        xb = tt % 3
        if tt + 2 < _fin:
            nb_ = (tt + 2) % 3
            P.dma("sp", I("dma_start", out=XS[:, nb_, :], in_=xp[(tt + 2) * 128:(tt + 3) * 128, :]),
                  "xs%d" % nb_, writes=["XS%d" % nb_])
        out_proj_ln(128, lambda ph, tt=tt: UT[:, ph, tt * 128:(tt + 1) * 128], wdst, XS[:, xb, :], "XS%d" % xb,
                    y_o[tt * 128:(tt + 1) * 128, :], "yo", 2 * (tt % 2), dq="pool")

    P.dma("sp", None, "fence", reads=["OUT"])
    fence = P.ops[-1]
    fence.dma = False
    fence.fn = lambda q: None
    fence.deps = sorted(set(fence.deps) | {op.idx for op in P.ops if "OUT" in op.writes})

    print('[build] sbuf bytes remaining', nc.sbuf_bytes_remaining, 'ops', len(P.ops))
    keys = P.plan()
    sems = {k: es.enter_context(nc.semaphore("s_%s_%s" % (k[0], k[1]))) for k in keys}
    block = es.enter_context(nc.Block())

    @block.sync
    def _(q):
        P.run(sems, "sp", q)

    @block.tensor
    def _(q):
        P.run(sems, "pe", q)

    @block.scalar
    def _(q):
        P.run(sems, "act", q)

    @block.vector
    def _(q):
        P.run(sems, "dve", q)

    @block.gpsimd
    def _(q):
        P.run(sems, "pool", q)

    es.close()
    return nc


def _tables(j, S):
    NSL = (S // 2) // 512
    NTO = (S // 2) // 128
    NTL = S // 128
    t = np.arange(512)
    tq = 512 * (t // 256) + 256 * j + (t % 256)
    p = np.arange(128)[:, None]
    maskf = np.zeros((128, 8, 512), np.float16)
    nega = np.zeros((128, 8, 512), np.float16)
    for z in range(8):
        zz = z % 4
        half = j if z < 4 else 1 - j
        pos = 512 * (zz // 2) + 256 * half + 128 * (zz % 2)
        s = pos + p
        maskf[:, z, :] = np.where(s <= tq[None, :], 0.0, NEG).astype(np.float16)
        ok = (s // 64) <= (tq[None, :] // 64)
        nega[:, z, :] = np.where(ok, -np.abs(tq[None, :] - s), NEG).astype(np.float16)
    negi = np.zeros((128, 2, 512), np.float16)
    negi[:, 0, :] = -(tq[None, :] - 256 * j - p)
    negi[:, 1, :] = -(tq[None, :] - 256 * (1 - j) - p)

    def nat(tl):
        own = tl < NTO
        to = tl % NTO
        half = j if own else 1 - j
        return 4 * (to // 2) + 2 * half + (to % 2)

    ncol = 128 + 2 * NSL
    l2 = np.zeros((128, ncol), np.float32)
    if 2 * NTL <= 128:
        for tl1 in range(NTL):
            for hh in range(2):
                for tl2 in range(NTL):
                    if nat(tl1) < nat(tl2):
                        l2[tl1 * 2 + hh, tl2 * 2 + hh] = 1.0
                for i in range(NSL):
                    if nat(tl1) < 8 * i:
                        l2[tl1 * 2 + hh, 2 * NTL + 2 * i + hh] = 1.0
    return maskf, nega, negi, l2


def _sample_tables():
    p = np.arange(128)[:, None]
    t = np.arange(TS)[None, :]
    smask = np.where(p <= t, 0.0, NEG)
    snega = (-np.abs(t - p)).astype(np.float16)
    snegi = (-(t - p)).astype(np.float16)
    l2s = np.zeros((128, 36), np.float32)
    for t1 in range(17):
        for hh in range(2):
            for t2 in range(17):
                if t1 < t2:
                    l2s[t1 * 2 + hh, t2 * 2 + hh] = 1.0
            if t1 < 16:
                l2s[t1 * 2 + hh, 34 + hh] = 1.0
    return smask.astype(np.float16), snega, snegi, l2s


_NC_CACHE = {}


def kernel(x_prompt, x_sample, cache_fox_k, cache_fox_v, cache_fox_logf, cache_diff_k, cache_diff_v,
           c_prompt, c_sample, w_ada, b_ada, w_in, b_f, lambda_q1, lambda_k1, lambda_q2, lambda_k2,
           subln_g, w_out, ln_g, ln_b):
    f = lambda a: np.ascontiguousarray(np.asarray(a, dtype=np.float32))
    x_prompt, x_sample = f(x_prompt), f(x_sample)
    B, S, _ = x_prompt.shape
    SO = S // 2
    if S not in _NC_CACHE:
        import os
        _NC_CACHE[S] = build(S, tuple(os.environ.get('KPARTS', 'sample,prompt').split(',')))
    nc = _NC_CACHE[S]
    smask, snega, snegi, l2s = _sample_tables()
    in_maps = []
    perms = []
    for c in range(NCORES):
        b, j = c // 2, c % 2
        blk = np.arange(S).reshape(S // 512, 2, 256)
        own = blk[:, j, :].reshape(-1)
        oth = blk[:, 1 - j, :].reshape(-1)
        perms.append(own)
        maskf, nega, negi, l2 = _tables(j, S)
        cc = np.concatenate([f(c_prompt)[b].reshape(8, 128).T, f(c_sample)[c].reshape(8, 128).T], axis=1)
        lamv = np.concatenate([f(lambda_q1)[0], f(lambda_k1)[0], f(lambda_q2)[0], f(lambda_k2)[0]])[None, :]
        in_maps.append({
            "xp": np.ascontiguousarray(x_prompt[b][np.concatenate([own, oth])]),
            "cc": np.ascontiguousarray(cc),
            "w_ada": f(w_ada)[0], "b_ada2": np.ascontiguousarray(np.repeat(f(b_ada), 2, axis=0)),
            "w_in": f(w_in)[0], "bf_bc": np.ascontiguousarray(np.repeat(f(b_f), 128, axis=0)),
            "lamv": np.ascontiguousarray(lamv), "sg_col": np.ascontiguousarray(f(subln_g)[0][:, None]),
            "lng_bc": np.ascontiguousarray(np.repeat(f(ln_g), 128, axis=0)),
            "lnb_bc": np.ascontiguousarray(np.repeat(f(ln_b), 128, axis=0)),
            "w_out": f(w_out)[0], "maskf": maskf, "nega": nega, "negi": negi, "l2": l2,
            "xsm": x_sample[c], "cfk": f(cache_fox_k)[0, c], "cfv": f(cache_fox_v)[0, c], "cfl": f(cache_fox_logf)[0, c],
            "cdk": np.ascontiguousarray(f(cache_diff_k)[0, c].reshape(4, PAST, 128)), "cdv": f(cache_diff_v)[0, c],
            "smask": smask, "snega": snega, "snegi": snegi, "l2s": l2s,
        })
    res = run_bass_kernel_spmd(nc, in_maps, core_ids=list(range(NCORES)))
    R = res.results
    yp = np.zeros((B, S, D), np.float32)
    pk_f = np.zeros((1, B, 8, S, 64), np.float32)
    pv_f = np.zeros((1, B, 8, S, 64), np.float32)
    plf = np.zeros((1, B, 8, S), np.float32)
    pk_d = np.zeros((1, B, 4, S, 2, 64), np.float32)
    pv_d = np.zeros((1, B, 4, S, 128), np.float32)
    ys = np.zeros((NCORES, TS, D), np.float32)
    sk_f = np.zeros((1, NCORES, 8, TS, 64), np.float32)
    sv_f = np.zeros((1, NCORES, 8, TS, 64), np.float32)
    slf = np.zeros((1, NCORES, 8, TS), np.float32)
    sk_d = np.zeros((1, NCORES, 4, TS, 2, 64), np.float32)
    sv_d = np.zeros((1, NCORES, 4, TS, 128), np.float32)
    for c in range(NCORES):
        b, own = c // 2, perms[c]
        r = R[c]
        yp[b, own] = r["y_o"]
        pk_f[0, b][:, own] = r["fk_o"]
        pv_f[0, b][:, own] = r["fv_o"]
        plf[0, b][:, own] = r["fl_o"]
        pk_d[0, b][:, own] = r["dk_o"].reshape(4, SO, 2, 64)
        pv_d[0, b][:, own] = r["dv_o"]
        ys[c] = r["ys_o"]
        sk_f[0, c] = r["sfk_o"]
        sv_f[0, c] = r["sfv_o"]
        slf[0, c] = r["sfl_o"]
        sk_d[0, c] = r["sdk_o"].reshape(4, TS, 2, 64)
        sv_d[0, c] = r["sdv_o"]
    return (yp, ys, pk_f, pv_f, plf, pk_d, pv_d, sk_f, sv_f, slf, sk_d, sv_d)
```
